# Optimizing a Trainium2 kernel written in Bass

```python
import jax, jax.numpy as jnp
from jax import lax
import numpy as np

D_MODEL = 1024
BATCH = 8
SEQ = 2048
DEPTH = 2

N_MIXERS = 2
N_MLSTM_LAYERS = (DEPTH + 1) // 2
N_POOL_LAYERS = DEPTH // 2
MLSTM_HEADS = 8
MLSTM_DV = D_MODEL // MLSTM_HEADS
MLSTM_DQK = MLSTM_DV // 2
MLSTM_CHUNK = 64
GATE_SOFTCAP = 15.0
MLSTM_IN_COLS = 2 * MLSTM_HEADS * MLSTM_DQK + 2 * MLSTM_HEADS * MLSTM_DV + 4 * MLSTM_HEADS
POOL_WINDOWS = (2, 4, 8, 16)
POOL_GROUPS = len(POOL_WINDOWS)
POOL_GROUP_DIM = D_MODEL // POOL_GROUPS
D_FF = 4 * D_MODEL
EPS = 1e-6

kernel_name = 'bidir_mlstm_pool_hybrid_trunk'


def _rmsnorm(x, g):
    xf = x.astype(jnp.float32)
    y = xf * lax.rsqrt(jnp.mean(xf * xf, axis=-1, keepdims=True) + EPS)
    return (y * g.astype(jnp.float32)).astype(x.dtype)


def _mlstm_chunkwise(q, k, v, log_i, log_f):
    B, H, S, dk = q.shape
    dv = v.shape[-1]
    L = MLSTM_CHUNK
    nc = S // L

    def to_chunks(t):
        return jnp.moveaxis(t.reshape((B, H, nc, L) + t.shape[3:]), 2, 0)

    xs = tuple(to_chunks(t) for t in (q, k, v, log_i, log_f))
    lower = jnp.tril(jnp.ones((L, L), dtype=bool))

    def step(carry, inp):
        C, n, m = carry
        qj, kj, vj, ij, fj = inp
        b = jnp.cumsum(fj, axis=-1)
        d = b[..., :, None] - b[..., None, :] + ij[..., None, :]
        d = jnp.where(lower, d, -jnp.inf)
        m_inter = b + m[..., None]
        m_t = jnp.maximum(m_inter, jnp.max(d, axis=-1))
        s = jnp.einsum('bhtd,bhsd->bhts', qj, kj) * jnp.exp(d - m_t[..., None])
        sc = jnp.exp(m_inter - m_t)
        num = jnp.einsum('bhts,bhsv->bhtv', s, vj) + sc[..., None] * jnp.einsum('bhtd,bhdv->bhtv', qj, C)
        den = jnp.sum(s, axis=-1) + sc * jnp.einsum('bhtd,bhd->bht', qj, n)
        h = num / jnp.maximum(jnp.abs(den), jnp.exp(-m_t))[..., None]
        b_last = b[..., -1]
        g = b_last[..., None] - b + ij
        m_new = jnp.maximum(b_last + m, jnp.max(g, axis=-1))
        decay = jnp.exp(b_last + m - m_new)
        wk = jnp.exp(g - m_new[..., None])
        C_new = decay[..., None, None] * C + jnp.einsum('bhs,bhsd,bhsv->bhdv', wk, kj, vj)
        n_new = decay[..., None] * n + jnp.einsum('bhs,bhsd->bhd', wk, kj)
        return (C_new, n_new, m_new), h

    init = (jnp.zeros((B, H, dk, dv), jnp.float32),
            jnp.zeros((B, H, dk), jnp.float32),
            jnp.zeros((B, H), jnp.float32))
    _, hc = lax.scan(step, init, xs)
    return jnp.moveaxis(hc, 0, 2).reshape(B, H, S, dv)


def _mlstm_mixer(u, w_in, gate_b, head_g, w_out):
    B, S, _ = u.shape
    H, dk, dv = MLSTM_HEADS, MLSTM_DQK, MLSTM_DV
    proj = u @ w_in
    cuts = [H * dk, 2 * H * dk, 2 * H * dk + H * dv, 2 * H * dk + 2 * H * dv]
    q, k, v, o, gates = jnp.split(proj, cuts, axis=-1)

    def heads(t, d):
        return t.reshape(B, S, H, d).transpose(0, 2, 1, 3).astype(jnp.float32)

    q = heads(q, dk) * (dk ** -0.5)
    k = heads(k, dk)
    v = heads(v, dv)
    g = gates.astype(jnp.float32) + gate_b.astype(jnp.float32)
    g = GATE_SOFTCAP * jnp.tanh(g / GATE_SOFTCAP)
    g = g.reshape(B, S, 4, H).transpose(2, 0, 3, 1)
    h_fwd = _mlstm_chunkwise(q, k, v, g[0], jax.nn.log_sigmoid(g[1]))

    def flip(t):
        return jnp.flip(t, axis=2)

    h_bwd = flip(_mlstm_chunkwise(flip(q), flip(k), flip(v), flip(g[2]), flip(jax.nn.log_sigmoid(g[3]))))
    h = h_fwd + h_bwd
    h = h * lax.rsqrt(jnp.mean(h * h, axis=-1, keepdims=True) + EPS)
    h = h.transpose(0, 2, 1, 3).reshape(B, S, H * dv) * head_g.astype(jnp.float32)
    h = (h * jax.nn.sigmoid(o.astype(jnp.float32))).astype(u.dtype)
    return h @ w_out


def _centred_mean(a, w):
    B, S, C = a.shape
    csum = jnp.concatenate([jnp.zeros((B, 1, C), jnp.float32),
                            jnp.cumsum(a.astype(jnp.float32), axis=1)], axis=1)
    t = np.arange(S)
    lo = np.clip(t - w // 2, 0, S)
    hi = np.clip(t + w - w // 2, 0, S)
    count = (hi - lo).astype(np.float32)
    return (csum[:, hi] - csum[:, lo]) / count[None, :, None]


def _pool_mixer(u, w_in, w_group, w_out, scale):
    B, S, _ = u.shape
    a = u @ w_in
    groups = jnp.split(a, POOL_GROUPS, axis=-1)
    pooled = jnp.stack([_centred_mean(gp, w) - gp.astype(jnp.float32)
                        for gp, w in zip(groups, POOL_WINDOWS)], axis=2)
    mixed = jnp.einsum('bsgc,gcd->bsgd', pooled.astype(u.dtype), w_group).reshape(B, S, D_MODEL)
    return (mixed @ w_out) * scale


def _mlp(u, w1, w2):
    return jnp.square(jax.nn.relu(u @ w1)) @ w2


def setup_inputs(seed: int = 0) -> dict:
    key = jax.random.key(seed)
    ks = jax.random.split(key, 16)
    D = D_MODEL

    def nrm(k, shape, fan_in):
        return jax.random.normal(k, shape, jnp.float32) * (fan_in ** -0.5)

    def gain(k, shape):
        return 1.0 + 0.02 * jax.random.normal(k, shape, jnp.float32)

    x = jax.random.normal(ks[0], (BATCH, SEQ, D), jnp.float32)
    mix_norm_g = gain(ks[1], (DEPTH, D))
    mlp_norm_g = gain(ks[2], (DEPTH, D))
    mlstm_w_in = nrm(ks[3], (N_MLSTM_LAYERS, D, MLSTM_IN_COLS), D)
    f_base = jnp.linspace(3.0, 6.0, MLSTM_HEADS, dtype=jnp.float32)
    zero = jnp.zeros_like(f_base)
    base = jnp.stack([zero, f_base, zero, f_base])[None]
    mlstm_gate_b = (base + 0.1 * jax.random.normal(ks[4], (N_MLSTM_LAYERS, 4, MLSTM_HEADS), jnp.float32)
                    ).reshape(N_MLSTM_LAYERS, 4 * MLSTM_HEADS)
    mlstm_head_g = gain(ks[5], (N_MLSTM_LAYERS, MLSTM_HEADS * MLSTM_DV))
    mlstm_w_out = nrm(ks[6], (N_MLSTM_LAYERS, MLSTM_HEADS * MLSTM_DV, D), MLSTM_HEADS * MLSTM_DV)
    pool_w_in = nrm(ks[7], (N_POOL_LAYERS, D, D), D)
    pool_w_group = nrm(ks[8], (N_POOL_LAYERS, POOL_GROUPS, POOL_GROUP_DIM, POOL_GROUP_DIM), POOL_GROUP_DIM)
    pool_w_out = nrm(ks[9], (N_POOL_LAYERS, D, D), D)
    pool_scale = gain(ks[10], (N_POOL_LAYERS, D))
    mlp_w1 = nrm(ks[11], (DEPTH, D, D_FF), D)
    mlp_w2 = nrm(ks[12], (DEPTH, D_FF, D), D_FF)
    final_norm_g = gain(ks[13], (D,))
    return {'x': x, 'mix_norm_g': mix_norm_g, 'mlp_norm_g': mlp_norm_g,
            'mlstm_w_in': mlstm_w_in, 'mlstm_gate_b': mlstm_gate_b, 'mlstm_head_g': mlstm_head_g,
            'mlstm_w_out': mlstm_w_out, 'pool_w_in': pool_w_in, 'pool_w_group': pool_w_group,
            'pool_w_out': pool_w_out, 'pool_scale': pool_scale, 'mlp_w1': mlp_w1, 'mlp_w2': mlp_w2,
            'final_norm_g': final_norm_g}


def reference(x, mix_norm_g, mlp_norm_g, mlstm_w_in, mlstm_gate_b, mlstm_head_g, mlstm_w_out,
              pool_w_in, pool_w_group, pool_w_out, pool_scale, mlp_w1, mlp_w2, final_norm_g):
    for i in range(DEPTH):
        j = i // N_MIXERS
        u = _rmsnorm(x, mix_norm_g[i])
        if i % N_MIXERS == 0:
            x = x + _mlstm_mixer(u, mlstm_w_in[j], mlstm_gate_b[j], mlstm_head_g[j], mlstm_w_out[j])
        else:
            x = x + _pool_mixer(u, pool_w_in[j], pool_w_group[j], pool_w_out[j], pool_scale[j])
        u = _rmsnorm(x, mlp_norm_g[i])
        x = x + _mlp(u, mlp_w1[i], mlp_w2[i])
    return _rmsnorm(x, final_norm_g)
```

```python
import numpy as np
from contextlib import ExitStack
import concourse.bass as bass
import concourse.mybir as mybir
from concourse.bass_utils import run_bass_kernel_spmd

F32 = mybir.dt.float32
BF16 = mybir.dt.bfloat16
AF = mybir.ActivationFunctionType
ALU = mybir.AluOpType
AX = mybir.AxisListType

S = 2048
D = 1024
NT = 16
KC = 8
DFF = 4096
EPS = 1e-6
NH = 8
DK = 64
DV = 128
SOFTCAP = 15.0
POOL_W = (2, 4, 8, 16)
NEGBIG = -30000.0


class Region:
    __slots__ = ("name", "w", "r")

    def __init__(self, name):
        self.name = name
        self.w = None
        self.r = {}


class Eng:
    def __init__(self, es, nc, h, name):
        self.h = h
        self.name = name
        self.sem = es.enter_context(nc.semaphore("sem_" + name))
        self.cnt = 0
        self.known = {}
        self.pending = False


class DSem:
    def __init__(self, es, nc, name):
        self.sem = es.enter_context(nc.semaphore(name))
        self.cnt = 0


class Ctx:
    def __init__(self, nc, es):
        self.nc = nc
        self.es = es
        self.pe = Eng(es, nc, nc.tensor, "pe")
        self.act = Eng(es, nc, nc.scalar, "act")
        self.dve = Eng(es, nc, nc.vector, "dve")
        self.pool = Eng(es, nc, nc.gpsimd, "pool")
        self.sp = Eng(es, nc, nc.sync, "sp")
        self.engs = [self.pe, self.act, self.dve, self.pool, self.sp]
        self.dsems = []
        self.nreg = 0

    def reg(self, name="r"):
        self.nreg += 1
        return Region(name)

    def regs(self, n, name="r"):
        return [self.reg(name) for _ in range(n)]

    def dsem(self, name):
        self._nd = getattr(self, "_nd", 0) + 1
        d = DSem(self.es, self.nc, "%s_u%d" % (name, self._nd))
        self.dsems.append(d)
        return d

    def _wait(self, eng, deps):
        best = {}
        for sem, val in deps:
            k = id(sem)
            if k not in best or best[k][1] < val:
                best[k] = (sem, val)
        for k, (sem, val) in best.items():
            if sem is eng.sem and val > eng.cnt:
                continue
            if eng.known.get(k, 0) < val:
                eng.h.wait_ge(sem, val)
                eng.known[k] = val

    def _deps(self, reads, writes):
        deps = []
        for r in reads:
            if r.w is not None:
                deps.append(r.w)
        for w in writes:
            if w.w is not None:
                deps.append(w.w)
            deps.extend(w.r.values())
        return deps

    def _update(self, tag, reads, writes):
        k = id(tag[0])
        for r in reads:
            if k not in r.r or r.r[k][1] < tag[1]:
                r.r[k] = tag
        for w in writes:
            w.w = tag
            w.r = {}

    def op(self, eng, fn, reads=(), writes=(), inc=True):
        self._wait(eng, self._deps(reads, writes))
        ins = fn()
        if inc:
            eng.cnt += 1
            ins.then_inc(eng.sem, 1)
            eng.pending = False
            tag = (eng.sem, eng.cnt)
        else:
            eng.pending = True
            tag = (eng.sem, eng.cnt + 1)
        self._update(tag, reads, writes)
        return ins

    def dma(self, eng, dsem, fn, reads=(), writes=()):
        self._wait(eng, self._deps(reads, writes))
        ins = fn()
        dsem.cnt += 16
        ins.then_inc(dsem.sem, 16)
        self._update((dsem.sem, dsem.cnt), reads, writes)
        return ins

    def barrier(self):
        tags = [(e.sem, e.cnt) for e in self.engs if e.cnt > 0]
        tags += [(d.sem, d.cnt) for d in self.dsems if d.cnt > 0]
        for e in self.engs:
            assert not e.pending
            self._wait(e, tags)


def pool_band_blocks():
    out = {}
    t = np.arange(S)
    for w in POOL_W:
        lo = np.clip(t - w // 2, 0, S)
        hi = np.clip(t + w - w // 2, 0, S)
        cnt = (hi - lo).astype(np.float64)

        def block(si, ti):
            blk = np.zeros((128, 128), np.float64)
            for tt in range(128):
                tg = ti * 128 + tt
                for sg in range(lo[tg], hi[tg]):
                    sl = sg - si * 128
                    if 0 <= sl < 128:
                        blk[sl, tt] += 1.0 / cnt[tg]
                if si == ti:
                    blk[tt, tt] -= 1.0
            return blk.astype(np.float32)

        out[w] = {
            "first": block(0, 0), "diag": block(5, 5), "last": block(15, 15),
            "sub": block(4, 5),
            "sup": block(6, 5),
        }
    return out


C16_IDENT = 0
C16_POOL = 128
C16_NEGF = C16_POOL + 20 * 128
C16_NEGB = C16_NEGF + 128
C16_ONES = C16_NEGB + 128
C16_N = C16_ONES + 128
POOL_KINDS = ("first", "diag", "last", "sub", "sup")

C32_TRIF = 0
C32_TRIB = 128
C32_MUF = 256
C32_MUB = 384
C32_ONES = 512
C32_IDENT = 640
C32_NEGF = 768
C32_NEGB = 896
C32_N = 1024


def make_consts():
    c16 = np.zeros((128, C16_N), np.float32)
    c16[:, C16_IDENT:C16_IDENT + 128] = np.eye(128, dtype=np.float32)
    pb = pool_band_blocks()
    for wi, w in enumerate(POOL_W):
        for ki, kind in enumerate(POOL_KINDS):
            o = C16_POOL + (wi * 5 + ki) * 128
            c16[:, o:o + 128] = pb[w][kind]
    r = np.arange(128)[:, None]
    t = np.arange(128)[None, :]
    c16[:, C16_NEGF:C16_NEGF + 128] = NEGBIG * (r > t)
    c16[:, C16_NEGB:C16_NEGB + 128] = NEGBIG * (r < t)
    c16[:, C16_ONES:C16_ONES + 128] = 1.0
    c32 = np.zeros((128, C32_N), np.float32)
    c32[:, C32_TRIF:C32_TRIF + 128] = (r <= t)
    c32[:, C32_TRIB:C32_TRIB + 128] = (r >= t)
    c32[:, C32_MUF:C32_MUF + 128] = (r > t)
    c32[:, C32_MUB:C32_MUB + 128] = (r < t)
    c32[:, C32_ONES:C32_ONES + 128] = 1.0
    c32[:, C32_IDENT:C32_IDENT + 128] = np.eye(128, dtype=np.float32)
    c32[:, C32_NEGF:C32_NEGF + 128] = NEGBIG * (r > t)
    c32[:, C32_NEGB:C32_NEGB + 128] = NEGBIG * (r < t)
    return c16, c32


class Builder:
    def __init__(self, phases=None, final=True):
        self.phases = phases if phases is not None else ["mix0", "mlp0", "mix1", "mlp1"]
        self.final = final
        nc = bass.Bass("TRN2", target_bir_lowering=False)
        self.nc = nc
        dt = nc.dram_tensor
        self.x_in = dt("x", [S, D], F32, kind="ExternalInput").ap()
        self.mix_g = dt("mix_norm_g", [2, D], F32, kind="ExternalInput").ap()
        self.mlp_g = dt("mlp_norm_g", [2, D], F32, kind="ExternalInput").ap()
        self.w_in = dt("mlstm_w_in", [D, 3104], F32, kind="ExternalInput").ap()
        self.gate_b = dt("mlstm_gate_b", [1, 32], F32, kind="ExternalInput").ap()
        self.head_g = dt("mlstm_head_g", [1, D], F32, kind="ExternalInput").ap()
        self.w_out = dt("mlstm_w_out", [D, D], F32, kind="ExternalInput").ap()
        self.p_win = dt("pool_w_in", [D, D], F32, kind="ExternalInput").ap()
        self.p_wg = dt("pool_w_group", [4, 256, 256], F32, kind="ExternalInput").ap()
        self.p_wout = dt("pool_w_out", [D, D], F32, kind="ExternalInput").ap()
        self.p_scale = dt("pool_scale", [1, D], F32, kind="ExternalInput").ap()
        self.w1 = dt("mlp_w1", [2, D, DFF], F32, kind="ExternalInput").ap()
        self.w2 = dt("mlp_w2", [2, DFF, D], F32, kind="ExternalInput").ap()
        self.fin_g = dt("final_norm_g", [1, D], F32, kind="ExternalInput").ap()
        self.c16_d = dt("c16", [128, C16_N], F32, kind="ExternalInput").ap()
        self.c32_d = dt("c32", [128, C32_N], F32, kind="ExternalInput").ap()
        self.y_out = dt("y", [S, D], F32, kind="ExternalOutput").ap()

    def _uid(self, name):
        self._n = getattr(self, "_n", 0) + 1
        return "%s_u%d" % (name, self._n)

    def sb(self, es, name, shape, dtype):
        return es.enter_context(self.nc.sbuf_tensor(self._uid(name), shape, dtype))

    def ps(self, es, name, shape, dtype=F32):
        return es.enter_context(self.nc.psum_tensor(self._uid(name), shape, dtype))

    def build(self):
        nc = self.nc
        with ExitStack() as es:
            cx = Ctx(nc, es)
            self.cx = cx
            self.X = self.sb(es, "X", [128, NT, D], F32)
            self.Xr = cx.regs(NT, "X")
            self.UT = self.sb(es, "UT", [128, KC, S], BF16)
            self.UTr = cx.regs(NT, "UT")
            self.C16 = self.sb(es, "C16", [128, 128], BF16)
            self.C32 = self.sb(es, "C32", [128, C32_N], F32)
            self.GBr = cx.reg("GB")
            self.SS = self.sb(es, "SS", [128, NT], F32)
            self.RSTD = self.sb(es, "RSTD", [128, NT], F32)
            self.SSall = cx.reg("SS")
            self.RSall = cx.reg("RSTD")
            self.cr = cx.reg("consts")
            ds_c = cx.dsem("ds_const")
            cx.dma(cx.pool, ds_c, lambda: nc.gpsimd.dma_start(out=self.C16[:], in_=self.c16_d[:, C16_IDENT:C16_IDENT + 128]), writes=[self.cr])
            cx.dma(cx.sp, ds_c, lambda: nc.sync.dma_start(out=self.C32[:], in_=self.c32_d), writes=[self.cr])
            self.ident = self.C16[:, 0:128]
            ds_x = [cx.dsem("ds_x%d" % i) for i in range(4)]
            xin = self.x_in.rearrange("(i p) d -> p i d", p=128)
            for i in range(NT):
                cx.dma(cx.sp, ds_x[i % 4], (lambda i=i: nc.sync.dma_start(out=self.X[:, i, :], in_=xin[:, i, :])),
                       writes=[self.Xr[i]])
            for ph in self.phases:
                l = int(ph[-1])
                if ph.startswith("mix"):
                    if l == 0:
                        self.norm_transpose(self.mix_g[l:l + 1, :])
                        cx.barrier()
                        self.mlstm_mixer()
                    else:
                        self.pool_mixer(self.mix_g[l:l + 1, :])
                else:
                    self.mlp(l, self.mlp_g[l:l + 1, :], fuse_final=(self.final and ph is self.phases[-1]))
                cx.barrier()
            self.final_out()
        return nc

    def load_gain(self, g_row):
        nc, cx = self.nc, self.cx
        if not hasattr(self, "ds_g"):
            self.ds_g = cx.dsem("ds_g")
        cx.dma(cx.sp, self.ds_g, lambda: nc.sync.dma_start(out=self.GB[:], in_=g_row[0, :].partition_broadcast(128)),
               writes=[self.GBr])

    def rstd_all(self, junk, junk_r):
        nc, cx = self.nc, self.cx
        nj = len(junk)
        for i in range(NT):
            b = i % nj
            cx.op(cx.act, lambda i=i, b=b: nc.scalar.activation(out=junk[b][:], in_=self.X[:, i, :], func=AF.Square,
                                                              accum_out=self.SS[:, i:i + 1]),
                  reads=[self.Xr[i]], writes=[junk_r[b], self.SSall])
        cx.op(cx.dve, lambda: nc.vector.tensor_scalar(out=self.RSTD[:], in0=self.SS[:], scalar1=1.0 / D, scalar2=EPS,
                                                      op0=ALU.mult, op1=ALU.add), reads=[self.SSall], writes=[self.RSall])
        cx.op(cx.act, lambda: nc.scalar.activation(out=self.RSTD[:], in_=self.RSTD[:], func=AF.Sqrt),
              reads=[self.RSall], writes=[self.RSall])
        cx.op(cx.dve, lambda: nc.vector.reciprocal(out=self.RSTD[:], in_=self.RSTD[:]), reads=[self.RSall], writes=[self.RSall])

    def norm_transpose(self, g_row):
        nc, cx = self.nc, self.cx
        with ExitStack() as es:
            self.GB = self.sb(es, "GB", [128, D], F32)
            self.load_gain(g_row)
            junk = [self.sb(es, "nt_junk%d" % j, [128, D], BF16) for j in range(2)]
            junk_r = cx.regs(2, "junk")
            NB = 3
            un = [self.sb(es, "nt_un%d" % j, [128, D], BF16) for j in range(NB)]
            un_r = cx.regs(NB, "un")
            tp = [self.ps(es, "nt_tp%d" % j, [128, KC, 128], BF16) for j in range(NB)]
            tp_r = cx.regs(NB, "tp")
            self.rstd_all(junk, junk_r)
            for i in range(NT):
                b = i % NB
                cx.op(cx.dve, lambda: nc.vector.scalar_tensor_tensor(
                    out=un[b][:], in0=self.X[:, i, :], scalar=self.RSTD[:, i:i + 1], in1=self.GB[:],
                    op0=ALU.mult, op1=ALU.mult),
                    reads=[self.Xr[i], self.RSall, self.GBr], writes=[un_r[b]])
                for k in range(KC):
                    cx.op(cx.pe, lambda k=k: nc.tensor.transpose(tp[b][:, k, :], un[b][:, k * 128:(k + 1) * 128],
                                                                 self.ident),
                          reads=[un_r[b], self.cr], writes=[tp_r[b]], inc=(k == KC - 1))
                cx.op(cx.act, lambda: nc.scalar.copy(out=self.UT[:, :, i * 128:(i + 1) * 128], in_=tp[b][:]),
                      reads=[tp_r[b]], writes=[self.UTr[i]])

    def mlp(self, l, g_row, fuse_final=False):
        nc, cx = self.nc, self.cx
        FG = 4
        NG = DFF // (128 * FG)
        w1v = self.w1[l].rearrange("(k p) f -> p k f", p=128)
        w2v = self.w2[l].rearrange("(c p) n -> p c n", p=128)
        with ExitStack() as es:
            W1 = [self.sb(es, "W1_%d" % j, [128, KC, FG * 128], BF16) for j in range(2)]
            W2 = [self.sb(es, "W2_%d" % j, [128, FG, D], BF16) for j in range(2)]
            W1r, W2r = cx.regs(2, "W1"), cx.regs(2, "W2")
            dW1 = [cx.dsem("ds_w1_%d_%d" % (l, j)) for j in range(2)]
            dW2 = [cx.dsem("ds_w2_%d_%d" % (l, j)) for j in range(2)]

            def load(g):
                b = g % 2
                cx.dma(cx.pool, dW1[b], lambda: nc.gpsimd.dma_start(out=W1[b][:], in_=w1v[:, :, g * FG * 128:(g + 1) * FG * 128]),
                       writes=[W1r[b]])
                cx.dma(cx.pool, dW2[b], lambda: nc.gpsimd.dma_start(out=W2[b][:], in_=w2v[:, g * FG:(g + 1) * FG, :]),
                       writes=[W2r[b]])
            load(0)
            load(1)
            self.norm_transpose(g_row)
            cx.barrier()
            HT = [self.sb(es, "HT%d" % j, [128, FG, S], BF16) for j in range(2)]
            HTr = [[cx.reg("HT") for _ in range(FG * 4)] for _ in range(2)]
            RL = [self.sb(es, "RL%d" % j, [128, 512], F32) for j in range(3)]
            RLr = cx.regs(3, "RL")
            P1 = [self.ps(es, "mp1_%d" % j, [128, 512], F32) for j in range(3)]
            P1r = cx.regs(3, "P1")
            P2 = [self.ps(es, "mp2_%d" % j, [128, 512], F32) for j in range(4)]
            P2r = cx.regs(4, "P2")
            fbufs = self.final_alloc(es) if fuse_final else None

            cnt1 = [0]

            def mm1(g):
                b = g % 2
                for fc in range(FG):
                    for tb in range(4):
                        j = cnt1[0] % 3
                        cnt1[0] += 1
                        for k in range(KC):
                            cx.op(cx.pe, lambda k=k: nc.tensor.matmul(
                                P1[j][:], W1[b][:, k, fc * 128:(fc + 1) * 128], self.UT[:, k, tb * 512:(tb + 1) * 512],
                                start=(k == 0), stop=(k == KC - 1)),
                                reads=[W1r[b]] + self.UTr[tb * 4:tb * 4 + 4], writes=[P1r[j]], inc=(k == KC - 1))
                        cx.op(cx.act, lambda: nc.scalar.activation(out=RL[j][:], in_=P1[j][:], func=AF.Relu),
                              reads=[P1r[j]], writes=[RLr[j]])
                        cx.op(cx.pool, lambda: nc.gpsimd.tensor_tensor(
                            out=HT[b][:, fc, tb * 512:(tb + 1) * 512], in0=RL[j][:], in1=RL[j][:], op=ALU.mult),
                            reads=[RLr[j]], writes=[HTr[b][fc * 4 + tb]])

            cnt2 = [0]

            def mm2(g):
                b = g % 2
                for i in range(NT):
                    for nh in range(2):
                        j = cnt2[0] % 4
                        cnt2[0] += 1
                        for fc in range(FG):
                            cx.op(cx.pe, lambda fc=fc: nc.tensor.matmul(
                                P2[j][:], HT[b][:, fc, i * 128:(i + 1) * 128], W2[b][:, fc, nh * 512:(nh + 1) * 512],
                                start=(fc == 0), stop=(fc == FG - 1)),
                                reads=[W2r[b], HTr[b][fc * 4 + i // 4]], writes=[P2r[j]], inc=(fc == FG - 1))
                        cx.op(cx.dve, lambda: nc.vector.tensor_tensor(
                            out=self.X[:, i, nh * 512:(nh + 1) * 512], in0=P2[j][:], in1=self.X[:, i, nh * 512:(nh + 1) * 512],
                            op=ALU.add),
                            reads=[P2r[j], self.Xr[i]], writes=[self.Xr[i]])

            mm1(0)
            for g in range(NG):
                if g + 1 < NG:
                    mm1(g + 1)
                mm2(g)
                if g + 2 < NG:
                    load(g + 2)
            if fuse_final:
                self.final_emit(fbufs)

    def final_alloc(self, es):
        cx = self.cx
        return dict(
            GB=self.sb(es, "GBf", [128, D], F32),
            junk=[self.sb(es, "fo_junk%d" % j, [128, D], BF16) for j in range(2)], junk_r=cx.regs(2, "junk"),
            yo=[self.sb(es, "fo_y%d" % j, [128, D], F32) for j in range(3)], yo_r=cx.regs(3, "yo"),
            ds_y=[cx.dsem("ds_y%d" % j) for j in range(3)])

    def final_emit(self, fb):
        nc, cx = self.nc, self.cx
        yv = self.y_out.rearrange("(i p) d -> p i d", p=128)
        ds_y, yo, yo_r = fb["ds_y"], fb["yo"], fb["yo_r"]
        self.GB = fb["GB"]
        self.load_gain(self.fin_g)
        self.rstd_all(fb["junk"], fb["junk_r"])
        for i in range(NT):
            b = i % 3
            cx.op(cx.dve, lambda: nc.vector.scalar_tensor_tensor(
                out=yo[b][:], in0=self.X[:, i, :], scalar=self.RSTD[:, i:i + 1], in1=self.GB[:],
                op0=ALU.mult, op1=ALU.mult),
                reads=[self.Xr[i], self.RSall, self.GBr], writes=[yo_r[b]])
            cx.dma(cx.sp, ds_y[b], lambda: nc.sync.dma_start(out=yv[:, i, :], in_=yo[b][:]), reads=[yo_r[b]])
        for d in ds_y:
            if d.cnt:
                nc.sync.wait_ge(d.sem, d.cnt)
        self._final_done = True

    def final_out(self):
        nc, cx = self.nc, self.cx
        if getattr(self, "_final_done", False):
            return
        yv = self.y_out.rearrange("(i p) d -> p i d", p=128)
        with ExitStack() as es:
            if not self.final:
                ds_y = [cx.dsem("ds_y%d" % j) for j in range(2)]
                for i in range(NT):
                    cx.dma(cx.sp, ds_y[i % 2], lambda i=i: nc.sync.dma_start(out=yv[:, i, :], in_=self.X[:, i, :]),
                           reads=[self.Xr[i]])
                for d in ds_y:
                    if d.cnt:
                        nc.sync.wait_ge(d.sem, d.cnt)
            else:
                self.final_emit(self.final_alloc(es))

    def mlstm_mixer(self):
        nc, cx = self.nc, self.cx
        wv = self.w_in.rearrange("(k p) n -> p k n", p=128)
        woutv = self.w_out.rearrange("(h p) n -> p h n", p=128)
        C32 = self.C32
        TRI2 = C32[:, C32_TRIF:C32_TRIF + 256].rearrange("p (d t) -> p d t", d=2)
        TRI = [C32[:, C32_TRIF:C32_TRIF + 128], C32[:, C32_TRIB:C32_TRIB + 128]]
        MU = [C32[:, C32_MUF:C32_MUF + 128], C32[:, C32_MUB:C32_MUB + 128]]
        NEG = [C32[:, C32_NEGF:C32_NEGF + 128], C32[:, C32_NEGB:C32_NEGB + 128]]
        ONES32 = C32[:, C32_ONES:C32_ONES + 128]
        ID32 = C32[:, C32_IDENT:C32_IDENT + 128]
        with ExitStack() as es:
            sb = lambda name, shape, dt: self.sb(es, "ml_" + name, shape, dt)
            PSB = [self.ps(es, "ml_bank%d" % j, [128, 512], F32) for j in range(8)]
            PSr = cx.regs(8, "bank")
            LI = sb("li", [128, NT, 2, 8], F32)
            LF = sb("lf", [128, NT, 2, 8], F32)
            G = sb("g", [128, NT, 2, 8], F32)
            WKK = sb("wkk", [128, NT, 2, 8], F32)
            GX = sb("gx", [128, NT, 2, 8], F32)
            GXP = sb("gxp", [128, NT, 2], F32)
            gr = cx.reg("gates")
            gxpr = cx.reg("gxp")
            WQK = sb("wqk", [128, KC, 256], BF16)
            WV = sb("wv", [128, KC, 256], BF16)
            WO = sb("wo", [128, KC, 256], BF16)
            WOUT = sb("wout", [128, 2, D], BF16)
            HGB = sb("hgb", [128, 256], F32)
            wr1, wr2 = cx.reg("mlw1"), cx.reg("mlw2")
            dsw, dsw1, dsw2 = cx.dsem("ds_mlw"), cx.dsem("ds_mlw1"), cx.dsem("ds_mlw2")
            QTbd = sb("qt", [128, 2, S], BF16)
            KT2 = sb("kt", [128, S], BF16)
            KTOK = sb("ktok", [128, NT, 128], BF16)
            VA = sb("va", [128, NT, 2, 129], BF16)
            H32 = sb("h32", [128, NT, 256], F32)
            PTt = [sb("pt%d" % j, [128, 2, 2, 128], BF16) for j in range(2)]
            CBall = [sb("cball%d" % d, [128, NT, 129], BF16) for d in range(2)]
            KWall = sb("kwall", [128, NT, 2, 128], BF16)
            QTr, KTr = cx.regs(4, "QT"), cx.regs(4, "KT")
            KTOKr, Vr, Hr, PTr = cx.regs(NT), cx.regs(NT), cx.regs(NT), cx.regs(2)
            CBAr = [cx.regs(NT) for _ in range(2)]
            KWr = [cx.regs(NT) for _ in range(2)]
            var1 = cx.reg("va_ones")
            LFM = [sb("lfm%d" % j, [128, 2, 2, 128], F32) for j in range(2)]
            AT = [sb("at%d" % j, [128, 2, 2, 128], F32) for j in range(2)]
            LFMr, ATr = cx.regs(2), cx.regs(2)
            NEG4 = sb("neg4", [128, 2, 2, 128], BF16)
            negr = cx.reg()
            es_g = ExitStack()
            WGt = self.sb(es_g, "ml_wg", [128, KC, 32], BF16)
            GBI = self.sb(es_g, "ml_gbi", [128, 32], F32)
            TH = self.sb(es_g, "ml_th", [128, NT, 32], F32)
            cx.dma(cx.pool, dsw, lambda: nc.gpsimd.dma_start(out=WGt[:], in_=wv[:, :, 3072:3104]), writes=[gr])
            cx.dma(cx.sp, dsw, lambda: nc.sync.dma_start(out=GBI[:], in_=self.gate_b[0, :].partition_broadcast(128)), writes=[gr])
            for d in range(2):
                cx.op(cx.pool, lambda d=d: nc.gpsimd.tensor_copy(out=NEG4[:, d], in_=NEG[d].unsqueeze(1).broadcast_to([128, 2, 128])),
                      reads=[self.cr], writes=[negr])
            cx.op(cx.pool, lambda: nc.gpsimd.memset(VA[:, :, :, 128:129], 1.0), writes=[var1])
            cx.op(cx.pool, lambda: nc.gpsimd.memset(QTbd[:], 0.0), writes=QTr)
            cx.op(cx.pool, lambda: nc.gpsimd.memset(CBall[0][:, 0, :], 0.0), writes=[CBAr[0][0]])
            cx.op(cx.pool, lambda: nc.gpsimd.memset(CBall[1][:, NT - 1, :], 0.0), writes=[CBAr[1][NT - 1]])

            def load_w1(p):
                h0 = 2 * p
                cx.dma(cx.pool, dsw1, lambda: nc.gpsimd.dma_start(out=WQK[:, :, 0:128], in_=wv[:, :, h0 * 64:h0 * 64 + 128]), writes=[wr1])
                cx.dma(cx.pool, dsw1, lambda: nc.gpsimd.dma_start(out=WQK[:, :, 128:256], in_=wv[:, :, 512 + h0 * 64:512 + h0 * 64 + 128]),
                       writes=[wr1])
                cx.dma(cx.pool, dsw1, lambda: nc.gpsimd.dma_start(out=WV[:], in_=wv[:, :, 1024 + h0 * 128:1024 + h0 * 128 + 256]), writes=[wr1])

            def load_w2(p):
                h0 = 2 * p
                cx.dma(cx.pool, dsw2, lambda: nc.gpsimd.dma_start(out=WO[:], in_=wv[:, :, 2048 + h0 * 128:2048 + h0 * 128 + 256]), writes=[wr2])
                cx.dma(cx.pool, dsw2, lambda: nc.gpsimd.dma_start(out=WOUT[:], in_=woutv[:, h0:h0 + 2, :]), writes=[wr2])
                cx.dma(cx.sp, dsw2, lambda: nc.sync.dma_start(out=HGB[:], in_=self.head_g[0, h0 * 128:h0 * 128 + 256].partition_broadcast(128)),
                       writes=[wr2])
            load_w1(0)
            load_w2(0)

            PG = PSB[0][:].rearrange("p (i g) -> p i g", g=32)
            for i in range(NT):
                for k in range(KC):
                    cx.op(cx.pe, lambda i=i, k=k: nc.tensor.matmul(PG[:, i, :], self.UT[:, k, i * 128:(i + 1) * 128], WGt[:, k, :],
                                                                   start=(k == 0), stop=(k == KC - 1)),
                          reads=[gr, self.UTr[i]], writes=[PSr[0]], inc=(i == NT - 1 and k == KC - 1))
            cx.op(cx.dve, lambda: nc.vector.tensor_tensor(out=TH[:], in0=PG, in1=GBI[:].unsqueeze(1).broadcast_to([128, NT, 32]),
                                                          op=ALU.add), writes=[PSr[0], gr])
            cx.op(cx.act, lambda: nc.scalar.activation(out=TH[:], in_=TH[:], func=AF.Tanh, scale=1.0 / SOFTCAP), writes=[gr])
            THv = TH[:].rearrange("p i (g h) -> p i g h", g=4)
            for d in range(2):
                cx.op(cx.dve, lambda d=d: nc.vector.tensor_scalar(out=LI[:, :, d, :], in0=THv[:, :, 2 * d, :], scalar1=SOFTCAP, scalar2=None,
                                                                  op0=ALU.mult), writes=[gr])
                cx.op(cx.act, lambda d=d: nc.scalar.activation(out=LF[:, :, d, :], in_=THv[:, :, 2 * d + 1, :], func=AF.Exp, scale=-SOFTCAP),
                      writes=[gr])
            cx.op(cx.dve, lambda: nc.vector.tensor_scalar(out=LF[:], in0=LF[:], scalar1=1.0, scalar2=None, op0=ALU.add), writes=[gr])
            cx.op(cx.act, lambda: nc.scalar.activation(out=LF[:], in_=LF[:], func=AF.Ln), writes=[gr])
            cx.op(cx.dve, lambda: nc.vector.tensor_scalar(out=LF[:], in0=LF[:], scalar1=-1.0, scalar2=None, op0=ALU.mult), writes=[gr])
            v3 = lambda ap: ap.rearrange("p (i h) -> p i h", h=8)
            for d in range(2):
                for n, lh in enumerate((TRI[d], MU[d], ONES32)):
                    cx.op(cx.pe, lambda lh=lh, n=n, d=d: nc.tensor.matmul(PSB[1 + d][:, n * 128:(n + 1) * 128], lh, LF[:, :, d, :],
                                                                          start=True, stop=True),
                          reads=[gr, self.cr], writes=[PSr[1 + d]], inc=(n == 2))
                cx.op(cx.act, lambda d=d: nc.scalar.activation(out=G[:, :, d, :], in_=v3(PSB[1 + d][:, 0:128]), func=AF.Exp),
                      writes=[PSr[1 + d], gr])
                cx.op(cx.dve, lambda d=d: nc.vector.tensor_tensor(out=WKK[:, :, d, :], in0=v3(PSB[1 + d][:, 128:256]), in1=LI[:, :, d, :],
                                                                  op=ALU.add), writes=[PSr[1 + d], gr])
                cx.op(cx.act, lambda d=d: nc.scalar.activation(out=WKK[:, :, d, :], in_=WKK[:, :, d, :], func=AF.Exp), writes=[gr])
                cx.op(cx.act, lambda d=d: nc.scalar.activation(out=GX[:, :, d, :], in_=v3(PSB[1 + d][:, 256:384]), func=AF.Exp),
                      writes=[PSr[1 + d], gr])

            es_g.close()
            cx.barrier()
            TMP = [sb("tmp%d" % d, [128, 2, 129], F32) for d in range(2)]
            NUMt = sb("numt", [128, 2, 2, 129], F32)
            NUM = [NUMt[:, 0], NUMt[:, 1]]
            DSt = sb("dst", [128, 8], F32)
            CS = [sb("cs%d" % d, [128, 129], F32) for d in range(2)]
            TMPr, NUMr, DSr, CSr = [cx.regs(2) for _ in range(4)]
            JK = [sb("jk%d" % j, [128, 128], BF16) for j in range(2)]
            HSt = sb("hst_stat", [128, NT, 2], F32)
            HR = sb("hr", [128, NT, 2], F32)
            SG = [sb("sg%d" % j, [128, 256], F32) for j in range(2)]
            HSB = [sb("hsb%d" % j, [128, 256], BF16) for j in range(2)]
            HST = [sb("hstt%d" % j, [128, 2, 128], BF16) for j in range(2)]
            JKr, SGr, HSBr, HSTr = cx.regs(2), cx.regs(2), cx.regs(2), cx.regs(2)
            hsr = cx.reg("hstat")

            import os as _os
            _stop = _os.environ.get("MLSTM_STOP", "all")
            for p in range(int(_os.environ.get("MLSTM_NP", "4")) if _stop != "gates" else 0):
                h0 = 2 * p
                hsl = slice(h0, h0 + 2)
                cnt = [0]

                def nxt():
                    j = cnt[0] % 4
                    cnt[0] += 1
                    return j
                def kv_tile(i):
                    j = nxt()
                    for k in range(KC):
                        cx.op(cx.pe, lambda k=k: nc.tensor.matmul(PSB[j][:, 256:384], self.UT[:, k, i * 128:(i + 1) * 128], WQK[:, k, 128:256],
                                                                  start=(k == 0), stop=(k == KC - 1)),
                              reads=[wr1, self.UTr[i]], writes=[PSr[j]], inc=False)
                    for k in range(KC):
                        cx.op(cx.pe, lambda k=k: nc.tensor.matmul(PSB[j][:, 0:256], self.UT[:, k, i * 128:(i + 1) * 128], WV[:, k, :],
                                                                  start=(k == 0), stop=(k == KC - 1)),
                              reads=[wr1, self.UTr[i]], writes=[PSr[j]], inc=(k == KC - 1))
                    cx.op(cx.act, lambda: nc.scalar.copy(out=KTOK[:, i, :], in_=PSB[j][:, 256:384]), writes=[PSr[j], KTOKr[i]])
                    cx.op(cx.dve, lambda: nc.vector.tensor_copy(out=VA[:, i, :, 0:128], in_=PSB[j][:, 0:256].rearrange("p (h v) -> p h v", h=2)),
                          writes=[PSr[j], Vr[i]])
                def kw_tile(i):
                    for d in range(2):
                        cx.op(cx.dve, lambda d=d: nc.vector.tensor_tensor(
                            out=KWall[:, i, d, :].rearrange("p (h k) -> p h k", h=2), in0=KTOK[:, i, :].rearrange("p (h k) -> p h k", h=2),
                            in1=WKK[:, i, d, hsl].unsqueeze(2).broadcast_to([128, 2, 64]), op=ALU.mult),
                            reads=[KTOKr[i], gr], writes=[KWr[d][i]])
                for hp in range(2):
                    cx.op(cx.dve, lambda hp=hp: nc.vector.tensor_copy(out=GXP[hp * 64:(hp + 1) * 64, :, :], in_=GX[hp * 64:(hp + 1) * 64, :, :, h0 + hp]),
                          reads=[gr], writes=[gxpr])
                def qk_group(which, tb):
                        j = nxt()
                        for k in range(KC):
                            cx.op(cx.pe, lambda k=k: nc.tensor.matmul(PSB[j][:], WQK[:, k, which * 128:(which + 1) * 128],
                                                                      self.UT[:, k, tb * 512:(tb + 1) * 512], start=(k == 0), stop=(k == KC - 1)),
                                  reads=[wr1] + self.UTr[tb * 4:tb * 4 + 4], writes=[PSr[j]], inc=(k == KC - 1))
                        if which == 0:
                            for hp in range(2):
                                cx.op(cx.act, lambda hp=hp: nc.scalar.mul(out=QTbd[hp * 64:(hp + 1) * 64, hp, tb * 512:(tb + 1) * 512],
                                                                          in_=PSB[j][hp * 64:(hp + 1) * 64, :], mul=DK ** -0.5),
                                      writes=[PSr[j], QTr[tb]])
                        else:
                            cx.op(cx.dve, lambda: nc.vector.tensor_copy(out=KT2[:, tb * 512:(tb + 1) * 512], in_=PSB[j][:]),
                                  writes=[PSr[j], KTr[tb]])
                DB = [6, 7]
                for d in range(2):
                    cx.op(cx.dve, lambda d=d: nc.vector.memset(CS[d][:], 0.0), writes=[CSr[d]])

                def emit_dc(stp, d):
                    c = stp if d == 0 else NT - 1 - stp
                    off = (stp % 2) * 256
                    for hp in range(2):
                        cx.op(cx.pe, lambda hp=hp: nc.tensor.matmul(PSB[DB[d]][hp * 64:(hp + 1) * 64, off:off + 129],
                                                                    KWall[:, c, d, hp * 64:(hp + 1) * 64], VA[:, c, hp, :], start=True, stop=True),
                              reads=[KWr[d][c], Vr[c], var1], writes=[PSr[DB[d]]], inc=(hp == 1))
                def s_step(stp):
                    for d in range(2):
                        c = stp if d == 0 else NT - 1 - stp
                        cn = c + 1 if d == 0 else c - 1
                        off = (stp % 2) * 256
                        if stp + 1 < NT - 1:
                            emit_dc(stp + 1, d)
                        cx.op(cx.dve, lambda: nc.vector.scalar_tensor_tensor(out=CS[d][:], in0=CS[d][:], scalar=GXP[:, c, d:d + 1],
                                                                             in1=PSB[DB[d]][:, off:off + 129], op0=ALU.mult, op1=ALU.add),
                              reads=[gxpr], writes=[PSr[DB[d]], CSr[d]])
                        cx.op(cx.act, lambda: nc.scalar.copy(out=CBall[d][:, cn, :], in_=CS[d][:]), reads=[CSr[d]], writes=[CBAr[d][cn]])
                for s8 in range(8):
                    for i in (s8, NT - 1 - s8):
                        kv_tile(i)
                        kw_tile(i)
                    if s8 == 0:
                        for d in range(2):
                            emit_dc(0, d)
                    if s8 >= 1:
                        s_step(s8 - 1)
                s_step(7)
                for g8 in range(8):
                    qk_group(g8 // 4, g8 % 4)
                    if 8 + g8 < NT - 1:
                        s_step(8 + g8)
                IB = [2, 4]
                NB_ = [3, 5]
                def emitA(c):
                    a = c % 2
                    tok = slice(c * 128, (c + 1) * 128)
                    tbk = c // 4
                    eb, sbk = a, 6 + a
                    cx.op(cx.pool, lambda: nc.gpsimd.tensor_tensor(out=LFM[a][:], in0=TRI2.unsqueeze(2).broadcast_to([128, 2, 2, 128]),
                                                                   in1=LF[:, c, :, hsl].unsqueeze(3).broadcast_to([128, 2, 2, 128]), op=ALU.mult),
                          reads=[gr, self.cr], writes=[LFMr[a]])
                    for d in range(2):
                        cx.op(cx.pe, lambda d=d: nc.tensor.matmul(PSB[eb][:, d * 256:(d + 1) * 256], MU[d], LFM[a][:, d], start=True, stop=False),
                              reads=[LFMr[a], self.cr], writes=[PSr[eb]], inc=False)
                        cx.op(cx.pe, lambda d=d: nc.tensor.matmul(PSB[eb][:, d * 256:(d + 1) * 256], self.ident, NEG4[:, d], start=False, stop=True),
                              reads=[negr, self.cr], writes=[PSr[eb]], inc=(d == 1))
                    cx.op(cx.pe, lambda: nc.tensor.matmul(PSB[sbk][:, 0:256], KT2[:, tok], QTbd[:, :, tok], start=True, stop=True),
                          reads=[KTr[tbk], QTr[tbk]], writes=[PSr[sbk]])
                    E4 = PSB[eb][:].rearrange("p (d h t) -> p d h t", d=2, h=2)
                    for d in range(2):
                        for hp in range(2):
                            cx.op(cx.act, lambda d=d, hp=hp: nc.scalar.activation(out=AT[a][:, d, hp, :], in_=E4[:, d, hp, :], func=AF.Exp,
                                                                                 bias=LI[:, c, d, h0 + hp:h0 + hp + 1]),
                                  reads=[gr], writes=[PSr[eb], ATr[a]])
                    cx.op(cx.dve, lambda: nc.vector.tensor_tensor(
                        out=PTt[a][:], in0=PSB[sbk][:, 0:256].rearrange("p (h t) -> p h t", h=2).unsqueeze(1).broadcast_to([128, 2, 2, 128]),
                        in1=AT[a][:], op=ALU.mult),
                        reads=[ATr[a]], writes=[PSr[sbk], PTr[a]])

                def emitB(c):
                    a = c % 2
                    tok = slice(c * 128, (c + 1) * 128)
                    tbk = c // 4
                    eb, sbk = a, 6 + a
                    for d in range(2):
                        for hp in range(2):
                            cx.op(cx.pe, lambda hp=hp: nc.tensor.matmul(PSB[IB[d]][:, hp * 256:hp * 256 + 129], PTt[a][:, d, hp, :], VA[:, c, hp, :],
                                                                        start=True, stop=True),
                                  reads=[PTr[a], Vr[c], var1], writes=[PSr[IB[d]]], inc=(hp == 1))
                        for hp in range(2):
                            cx.op(cx.pe, lambda hp=hp: nc.tensor.matmul(PSB[NB_[d]][:, hp * 256:hp * 256 + 129], QTbd[:, hp, tok],
                                                                        CBall[d][:, c, :], start=True, stop=True),
                                  reads=[QTr[tbk], CBAr[d][c]], writes=[PSr[NB_[d]]], inc=(hp == 1))
                    for d in range(2):
                        iv = PSB[NB_[d]][:].rearrange("p (h x) -> p h x", h=2)[:, :, 0:129]
                        nv = PSB[IB[d]][:].rearrange("p (h x) -> p h x", h=2)[:, :, 0:129]
                        for hp in range(2):
                            cx.op(cx.act, lambda hp=hp: nc.scalar.mul(out=TMP[d][:, hp, :], in_=iv[:, hp, :],
                                                                      mul=G[:, c, d, h0 + hp:h0 + hp + 1]),
                                  reads=[gr], writes=[PSr[NB_[d]], TMPr[d]])
                        cx.op(cx.dve, lambda: nc.vector.tensor_tensor(out=NUM[d], in0=nv, in1=TMP[d][:], op=ALU.add),
                              reads=[TMPr[d]], writes=[PSr[IB[d]], NUMr[d]])
                    den = NUMt[:, :, :, 128]
                    dv = lambda lo: DSt[:, lo:lo + 4].rearrange("p (d h) -> p d h", d=2)
                    cx.op(cx.dve, lambda: nc.vector.scalar_tensor_tensor(out=dv(0), in0=den, scalar=-1.0, in1=den, op0=ALU.mult, op1=ALU.max),
                          reads=[NUMr[0], NUMr[1]], writes=[DSr[0]])
                    cx.op(cx.dve, lambda: nc.vector.tensor_scalar(out=DSt[:, 0:4], in0=DSt[:, 0:4], scalar1=1.0, scalar2=None, op0=ALU.max),
                          writes=[DSr[0]])
                    cx.op(cx.dve, lambda: nc.vector.reciprocal(out=DSt[:, 4:8], in_=DSt[:, 0:4]), writes=[DSr[0]])
                    h3 = H32[:, c, :].rearrange("p (h v) -> p h v", h=2)
                    cx.op(cx.dve, lambda: nc.vector.tensor_tensor(out=h3, in0=NUMt[:, 0, :, 0:128],
                                                                  in1=DSt[:, 4:6].unsqueeze(2).broadcast_to([128, 2, 128]), op=ALU.mult),
                          reads=[NUMr[0], DSr[0]], writes=[Hr[c]])
                    for hp in range(2):
                        cx.op(cx.dve, lambda hp=hp: nc.vector.scalar_tensor_tensor(
                            out=h3[:, hp, :], in0=NUMt[:, 1, hp, 0:128], scalar=DSt[:, 6 + hp:7 + hp], in1=h3[:, hp, :],
                            op0=ALU.mult, op1=ALU.add),
                            reads=[NUMr[1], DSr[0]], writes=[Hr[c]])
                def stat_tile(i):
                    for hp in range(2):
                        cx.op(cx.act, lambda hp=hp: nc.scalar.activation(out=JK[hp][:], in_=H32[:, i, hp * 128:(hp + 1) * 128], func=AF.Square,
                                                                         accum_out=HSt[:, i, hp:hp + 1]),
                              reads=[Hr[i]], writes=[JKr[hp], hsr])
                emitA(0)
                for c in range(NT):
                    if c + 1 < NT:
                        emitA(c + 1)
                    if c == NT - 2 and p + 1 < 4:
                        load_w1(p + 1)
                    emitB(c)
                for i in range(NT):
                    stat_tile(i)
                cx.op(cx.dve, lambda: nc.vector.tensor_scalar(out=HR[:], in0=HSt[:], scalar1=1.0 / DV, scalar2=EPS, op0=ALU.mult, op1=ALU.add),
                      writes=[hsr])
                cx.op(cx.act, lambda: nc.scalar.activation(out=HR[:], in_=HR[:], func=AF.Sqrt), writes=[hsr])
                cx.op(cx.dve, lambda: nc.vector.reciprocal(out=HR[:], in_=HR[:]), writes=[hsr])
                for g4 in range(4):
                    sl = slice(g4 * 4, g4 * 4 + 4)
                    hv = H32[:, sl, :].rearrange("p c (h v) -> p c h v", h=2)
                    cx.op(cx.dve, lambda: nc.vector.tensor_tensor(out=hv, in0=hv, in1=HR[:, sl, :].unsqueeze(3).broadcast_to([128, 4, 2, 128]),
                                                                  op=ALU.mult), reads=[hsr], writes=Hr[sl])

                def st_o(i):
                    b = i % 2
                    j = i % 2
                    for k in range(KC):
                        cx.op(cx.pe, lambda k=k: nc.tensor.matmul(PSB[j][:, 0:256], self.UT[:, k, i * 128:(i + 1) * 128], WO[:, k, :],
                                                                  start=(k == 0), stop=(k == KC - 1)),
                              reads=[wr2, self.UTr[i]], writes=[PSr[j]], inc=(k == KC - 1))
                    cx.op(cx.act, lambda: nc.scalar.activation(out=SG[b][:], in_=PSB[j][:, 0:256], func=AF.Sigmoid), writes=[PSr[j], SGr[b]])
                    cx.op(cx.pool, lambda: nc.gpsimd.tensor_tensor(out=SG[b][:], in0=SG[b][:], in1=HGB[:], op=ALU.mult),
                          reads=[wr2], writes=[SGr[b]])
                    cx.op(cx.pool, lambda: nc.gpsimd.tensor_tensor(out=HSB[b][:], in0=H32[:, i, :], in1=SG[b][:], op=ALU.mult),
                          reads=[Hr[i], SGr[b]], writes=[HSBr[b]])

                def st_t(i):
                    b = i % 2
                    j = 2 + i % 2
                    for hp in range(2):
                        cx.op(cx.pe, lambda hp=hp: nc.tensor.matmul(PSB[j][:, hp * 128:(hp + 1) * 128], HSB[b][:, hp * 128:(hp + 1) * 128], self.ident,
                                                                    start=True, stop=True),
                              reads=[HSBr[b], self.cr], writes=[PSr[j]], inc=(hp == 1))
                    cx.op(cx.act, lambda: nc.scalar.copy(out=HST[b][:], in_=PSB[j][:, 0:256].rearrange("p (h t) -> p h t", h=2)),
                          writes=[PSr[j], HSTr[b]])

                def st_x(i):
                    b = i % 2
                    for nh in range(2):
                        j = 4 + (i * 2 + nh) % 4
                        for hp in range(2):
                            cx.op(cx.pe, lambda hp=hp: nc.tensor.matmul(PSB[j][:], HST[b][:, hp, :], WOUT[:, hp, nh * 512:(nh + 1) * 512],
                                                                        start=(hp == 0), stop=(hp == 1)),
                                  reads=[HSTr[b], wr2], writes=[PSr[j]], inc=(hp == 1))
                        cx.op(cx.dve, lambda: nc.vector.tensor_tensor(out=self.X[:, i, nh * 512:(nh + 1) * 512], in0=PSB[j][:],
                                                                      in1=self.X[:, i, nh * 512:(nh + 1) * 512], op=ALU.add),
                              writes=[PSr[j], self.Xr[i]])
                for n in range(NT + 2):
                    if n < NT:
                        st_o(n)
                    if 1 <= n < NT + 1:
                        st_t(n - 1)
                    if n >= 2:
                        st_x(n - 2)
                if p + 1 < 4:
                    load_w2(p + 1)

    def pool_mixer(self, g_row):
        nc, cx = self.nc, self.cx
        winv = self.p_win.rearrange("(k p) n -> p k n", p=128)
        woutv = self.p_wout.rearrange("(k p) n -> p k n", p=128)
        wgv = self.p_wg.rearrange("g (c p) n -> p (g c) n", p=128)
        with ExitStack() as es:
            WI = self.sb(es, "pm_wi", [128, KC, D], BF16)
            WG = self.sb(es, "pm_wg", [128, KC, 256], BF16)
            WO = WI
            SCB = self.sb(es, "pm_scb", [128, D], F32)
            PB16 = self.sb(es, "pm_pb16", [128, 20 * 128], BF16)
            pbr = cx.reg("pb16")
            WIr, WGr, SCr = cx.reg(), cx.reg(), cx.reg()
            WOr = WIr
            dsw = [cx.dsem("ds_pm%d" % j) for j in range(4)]
            cx.dma(cx.pool, cx.dsem("ds_pb16"), lambda: nc.gpsimd.dma_start(out=PB16[:], in_=self.c16_d[:, C16_POOL:C16_POOL + 20 * 128]), writes=[pbr])
            cx.dma(cx.pool, dsw[0], lambda: nc.gpsimd.dma_start(out=WI[:], in_=winv), writes=[WIr])
            cx.dma(cx.pool, dsw[1], lambda: nc.gpsimd.dma_start(out=WG[:], in_=wgv), writes=[WGr])
            cx.dma(cx.sp, dsw[3], lambda: nc.sync.dma_start(out=SCB[:], in_=self.p_scale[0, :].partition_broadcast(128)),
                   writes=[SCr])
            self.norm_transpose(g_row)
            cx.barrier()
            A = self.sb(es, "pm_a", [128, NT, D], BF16)
            PT = self.sb(es, "pm_pt", [128, KC, S], BF16)
            TMPs = [self.sb(es, "pm_tmp%d" % j, [128, 512], F32) for j in range(2)]
            Ar = cx.regs(NT, "A")
            PTr = [[cx.reg() for _ in range(4)] for _ in range(KC)]
            TMr = cx.regs(2)
            PS = [self.ps(es, "pm_ps%d" % j, [128, 512], F32) for j in range(4)]
            PSr = cx.regs(4)
            cnt = [0]

            def nxt():
                j = cnt[0] % 4
                cnt[0] += 1
                return j
            for i in range(NT):
                for nh in range(2):
                    j = nxt()
                    for k in range(KC):
                        cx.op(cx.pe, lambda k=k: nc.tensor.matmul(
                            PS[j][:], self.UT[:, k, i * 128:(i + 1) * 128], WI[:, k, nh * 512:(nh + 1) * 512],
                            start=(k == 0), stop=(k == KC - 1)),
                            reads=[WIr, self.UTr[i]], writes=[PSr[j]], inc=(k == KC - 1))
                    cx.op(cx.act, lambda: nc.scalar.copy(out=A[:, i, nh * 512:(nh + 1) * 512], in_=PS[j][:]),
                          reads=[PSr[j]], writes=[Ar[i]])
            cx.dma(cx.pool, dsw[2], lambda: nc.gpsimd.dma_start(out=WO[:], in_=woutv), writes=[WOr])
            for cc in range(KC):
                wi = cc // 2

                def blk(kind):
                    o = (wi * 5 + POOL_KINDS.index(kind)) * 128
                    return PB16[:, o:o + 128]
                for tb in range(4):
                    j = nxt()
                    for ii in range(4):
                        i = tb * 4 + ii
                        terms = []
                        if i > 0:
                            terms.append((i - 1, "sub"))
                        terms.append((i, "first" if i == 0 else ("last" if i == NT - 1 else "diag")))
                        if i < NT - 1:
                            terms.append((i + 1, "sup"))
                        for n, (si, kind) in enumerate(terms):
                            cx.op(cx.pe, lambda si=si, kind=kind, n=n: nc.tensor.matmul(
                                PS[j][:, ii * 128:(ii + 1) * 128], A[:, si, cc * 128:(cc + 1) * 128], blk(kind),
                                start=(n == 0), stop=(n == len(terms) - 1)),
                                reads=[Ar[si], pbr], writes=[PSr[j]], inc=(ii == 3 and n == len(terms) - 1))
                    cx.op(cx.act, lambda: nc.scalar.copy(out=PT[:, cc, tb * 512:(tb + 1) * 512], in_=PS[j][:]),
                          reads=[PSr[j]], writes=[PTr[cc][tb]])
            for g in range(4):
                for dc in range(2):
                    for tb in range(4):
                        j = nxt()
                        for ci in range(2):
                            cx.op(cx.pe, lambda ci=ci: nc.tensor.matmul(
                                PS[j][:], WG[:, g * 2 + ci, dc * 128:(dc + 1) * 128], PT[:, g * 2 + ci, tb * 512:(tb + 1) * 512],
                                start=(ci == 0), stop=(ci == 1)),
                                reads=[WGr, PTr[g * 2 + ci][tb]], writes=[PSr[j]], inc=(ci == 1))
                        cx.op(cx.act, lambda: nc.scalar.copy(out=self.UT[:, g * 2 + dc, tb * 512:(tb + 1) * 512], in_=PS[j][:]),
                              reads=[PSr[j]], writes=self.UTr[tb * 4:tb * 4 + 4])
            for i in range(NT):
                for nh in range(2):
                    j = nxt()
                    b = (i * 2 + nh) % 2
                    for k in range(KC):
                        cx.op(cx.pe, lambda k=k: nc.tensor.matmul(
                            PS[j][:], self.UT[:, k, i * 128:(i + 1) * 128], WO[:, k, nh * 512:(nh + 1) * 512],
                            start=(k == 0), stop=(k == KC - 1)),
                            reads=[WOr, self.UTr[i]], writes=[PSr[j]], inc=(k == KC - 1))
                    cx.op(cx.dve, lambda: nc.vector.tensor_tensor(out=TMPs[b][:], in0=PS[j][:], in1=SCB[:, nh * 512:(nh + 1) * 512],
                                                                  op=ALU.mult),
                          reads=[PSr[j], SCr], writes=[TMr[b]])
                    cx.op(cx.pool, lambda: nc.gpsimd.tensor_tensor(
                        out=self.X[:, i, nh * 512:(nh + 1) * 512], in0=self.X[:, i, nh * 512:(nh + 1) * 512], in1=TMPs[b][:],
                        op=ALU.add),
                        reads=[TMr[b], self.Xr[i]], writes=[self.Xr[i]])


_W_NAMES = ["mix_norm_g", "mlp_norm_g", "mlstm_w_in", "mlstm_gate_b", "mlstm_head_g", "mlstm_w_out", "pool_w_in",
            "pool_w_group", "pool_scale", "pool_w_out", "mlp_w1", "mlp_w2", "final_norm_g"]


def _in_maps(inputs, xs):
    c16, c32 = make_consts()
    base = {
        "mix_norm_g": inputs["mix_norm_g"], "mlp_norm_g": inputs["mlp_norm_g"],
        "mlstm_w_in": inputs["mlstm_w_in"][0], "mlstm_gate_b": inputs["mlstm_gate_b"],
        "mlstm_head_g": inputs["mlstm_head_g"], "mlstm_w_out": inputs["mlstm_w_out"][0],
        "pool_w_in": inputs["pool_w_in"][0], "pool_w_group": inputs["pool_w_group"][0],
        "pool_w_out": inputs["pool_w_out"][0], "pool_scale": inputs["pool_scale"],
        "mlp_w1": inputs["mlp_w1"], "mlp_w2": inputs["mlp_w2"],
        "final_norm_g": inputs["final_norm_g"].reshape(1, D),
        "c16": c16, "c32": c32,
    }
    base = {k: np.ascontiguousarray(v, dtype=np.float32) for k, v in base.items()}
    return [dict(base, x=np.ascontiguousarray(x, dtype=np.float32)) for x in xs]


def run(inputs, xs, phases=None, final=True, trace=False):
    nc = Builder(phases, final).build()
    res = run_bass_kernel_spmd(nc, _in_maps(inputs, xs), core_ids=list(range(len(xs))), trace=trace)
    return [r["y"] for r in res.results], res


def kernel(**inputs):
    x = np.asarray(inputs["x"], dtype=np.float32)
    ys, _ = run(inputs, [x[b] for b in range(x.shape[0])])
    return np.stack(ys, axis=0).astype(np.float32)
```

```python
import numpy as np
from contextlib import ExitStack
import concourse.bass as bass
import concourse.mybir as mybir
from concourse.bass_utils import run_bass_kernel_spmd

F32 = mybir.dt.float32
BF16 = mybir.dt.bfloat16
AF = mybir.ActivationFunctionType
ALU = mybir.AluOpType
AX = mybir.AxisListType

S = 2048
D = 1024
NT = 16
KC = 8
DFF = 4096
EPS = 1e-6
NH = 8
DK = 64
DV = 128
SOFTCAP = 15.0
POOL_W = (2, 4, 8, 16)
NEGBIG = -30000.0


class Region:
    __slots__ = ("name", "w", "r")

    def __init__(self, name):
        self.name = name
        self.w = None
        self.r = {}


class Eng:
    def __init__(self, es, nc, h, name):
        self.h = h
        self.name = name
        self.sem = es.enter_context(nc.semaphore("sem_" + name))
        self.cnt = 0
        self.known = {}
        self.pending = False


class DSem:
    def __init__(self, es, nc, name):
        self.sem = es.enter_context(nc.semaphore(name))
        self.cnt = 0


class Ctx:
    def __init__(self, nc, es):
        self.nc = nc
        self.es = es
        self.pe = Eng(es, nc, nc.tensor, "pe")
        self.act = Eng(es, nc, nc.scalar, "act")
        self.dve = Eng(es, nc, nc.vector, "dve")
        self.pool = Eng(es, nc, nc.gpsimd, "pool")
        self.sp = Eng(es, nc, nc.sync, "sp")
        self.engs = [self.pe, self.act, self.dve, self.pool, self.sp]
        self.dsems = []
        self.nreg = 0

    def reg(self, name="r"):
        self.nreg += 1
        return Region(name)

    def regs(self, n, name="r"):
        return [self.reg(name) for _ in range(n)]

    def dsem(self, name):
        self._nd = getattr(self, "_nd", 0) + 1
        d = DSem(self.es, self.nc, "%s_u%d" % (name, self._nd))
        self.dsems.append(d)
        return d

    def _wait(self, eng, deps):
        best = {}
        for sem, val in deps:
            k = id(sem)
            if k not in best or best[k][1] < val:
                best[k] = (sem, val)
        for k, (sem, val) in best.items():
            if sem is eng.sem and val > eng.cnt:
                continue
            if eng.known.get(k, 0) < val:
                eng.h.wait_ge(sem, val)
                eng.known[k] = val

    def _deps(self, reads, writes):
        deps = []
        for r in reads:
            if r.w is not None:
                deps.append(r.w)
        for w in writes:
            if w.w is not None:
                deps.append(w.w)
            deps.extend(w.r.values())
        return deps

    def _update(self, tag, reads, writes):
        k = id(tag[0])
        for r in reads:
            if k not in r.r or r.r[k][1] < tag[1]:
                r.r[k] = tag
        for w in writes:
            w.w = tag
            w.r = {}

    def op(self, eng, fn, reads=(), writes=(), inc=True):
        self._wait(eng, self._deps(reads, writes))
        ins = fn()
        if inc:
            eng.cnt += 1
            ins.then_inc(eng.sem, 1)
            eng.pending = False
            tag = (eng.sem, eng.cnt)
        else:
            eng.pending = True
            tag = (eng.sem, eng.cnt + 1)
        self._update(tag, reads, writes)
        return ins

    def dma(self, eng, dsem, fn, reads=(), writes=()):
        self._wait(eng, self._deps(reads, writes))
        ins = fn()
        dsem.cnt += 16
        ins.then_inc(dsem.sem, 16)
        self._update((dsem.sem, dsem.cnt), reads, writes)
        return ins

    def barrier(self):
        tags = [(e.sem, e.cnt) for e in self.engs if e.cnt > 0]
        tags += [(d.sem, d.cnt) for d in self.dsems if d.cnt > 0]
        for e in self.engs:
            assert not e.pending
            self._wait(e, tags)


def pool_band_blocks():
    out = {}
    t = np.arange(S)
    for w in POOL_W:
        lo = np.clip(t - w // 2, 0, S)
        hi = np.clip(t + w - w // 2, 0, S)
        cnt = (hi - lo).astype(np.float64)

        def block(si, ti):
            blk = np.zeros((128, 128), np.float64)
            for tt in range(128):
                tg = ti * 128 + tt
                for sg in range(lo[tg], hi[tg]):
                    sl = sg - si * 128
                    if 0 <= sl < 128:
                        blk[sl, tt] += 1.0 / cnt[tg]
                if si == ti:
                    blk[tt, tt] -= 1.0
            return blk.astype(np.float32)

        out[w] = {
            "first": block(0, 0), "diag": block(5, 5), "last": block(15, 15),
            "sub": block(4, 5),
            "sup": block(6, 5),
        }
    return out


C16_IDENT = 0
C16_POOL = 128
C16_NEGF = C16_POOL + 20 * 128
C16_NEGB = C16_NEGF + 128
C16_ONES = C16_NEGB + 128
C16_N = C16_ONES + 128
POOL_KINDS = ("first", "diag", "last", "sub", "sup")

C32_TRIF = 0
C32_TRIB = 128
C32_MUF = 256
C32_MUB = 384
C32_ONES = 512
C32_IDENT = 640
C32_NEGF = 768
C32_NEGB = 896
C32_N = 1024


def make_consts():
    c16 = np.zeros((128, C16_N), np.float32)
    c16[:, C16_IDENT:C16_IDENT + 128] = np.eye(128, dtype=np.float32)
    pb = pool_band_blocks()
    for wi, w in enumerate(POOL_W):
        for ki, kind in enumerate(POOL_KINDS):
            o = C16_POOL + (wi * 5 + ki) * 128
            c16[:, o:o + 128] = pb[w][kind]
    r = np.arange(128)[:, None]
    t = np.arange(128)[None, :]
    c16[:, C16_NEGF:C16_NEGF + 128] = NEGBIG * (r > t)
    c16[:, C16_NEGB:C16_NEGB + 128] = NEGBIG * (r < t)
    c16[:, C16_ONES:C16_ONES + 128] = 1.0
    c32 = np.zeros((128, C32_N), np.float32)
    c32[:, C32_TRIF:C32_TRIF + 128] = (r <= t)
    c32[:, C32_TRIB:C32_TRIB + 128] = (r >= t)
    c32[:, C32_MUF:C32_MUF + 128] = (r > t)
    c32[:, C32_MUB:C32_MUB + 128] = (r < t)
    c32[:, C32_ONES:C32_ONES + 128] = 1.0
    c32[:, C32_IDENT:C32_IDENT + 128] = np.eye(128, dtype=np.float32)
    c32[:, C32_NEGF:C32_NEGF + 128] = NEGBIG * (r > t)
    c32[:, C32_NEGB:C32_NEGB + 128] = NEGBIG * (r < t)
    return c16, c32


class Builder:
    def __init__(self, phases=None, final=True):
        self.phases = phases if phases is not None else ["mix0", "mlp0", "mix1", "mlp1"]
        self.final = final
        nc = bass.Bass("TRN2", target_bir_lowering=False)
        self.nc = nc
        dt = nc.dram_tensor
        self.x_in = dt("x", [S, D], F32, kind="ExternalInput").ap()
        self.mix_g = dt("mix_norm_g", [2, D], F32, kind="ExternalInput").ap()
        self.mlp_g = dt("mlp_norm_g", [2, D], F32, kind="ExternalInput").ap()
        self.w_in = dt("mlstm_w_in", [D, 3104], F32, kind="ExternalInput").ap()
        self.gate_b = dt("mlstm_gate_b", [1, 32], F32, kind="ExternalInput").ap()
        self.head_g = dt("mlstm_head_g", [1, D], F32, kind="ExternalInput").ap()
        self.w_out = dt("mlstm_w_out", [D, D], F32, kind="ExternalInput").ap()
        self.p_win = dt("pool_w_in", [D, D], F32, kind="ExternalInput").ap()
        self.p_wg = dt("pool_w_group", [4, 256, 256], F32, kind="ExternalInput").ap()
        self.p_wout = dt("pool_w_out", [D, D], F32, kind="ExternalInput").ap()
        self.p_scale = dt("pool_scale", [1, D], F32, kind="ExternalInput").ap()
        self.w1 = dt("mlp_w1", [2, D, DFF], F32, kind="ExternalInput").ap()
        self.w2 = dt("mlp_w2", [2, DFF, D], F32, kind="ExternalInput").ap()
        self.fin_g = dt("final_norm_g", [1, D], F32, kind="ExternalInput").ap()
        self.c16_d = dt("c16", [128, C16_N], F32, kind="ExternalInput").ap()
        self.c32_d = dt("c32", [128, C32_N], F32, kind="ExternalInput").ap()
        self.y_out = dt("y", [S, D], F32, kind="ExternalOutput").ap()

    def _uid(self, name):
        self._n = getattr(self, "_n", 0) + 1
        return "%s_u%d" % (name, self._n)

    def sb(self, es, name, shape, dtype):
        return es.enter_context(self.nc.sbuf_tensor(self._uid(name), shape, dtype))

    def ps(self, es, name, shape, dtype=F32):
        return es.enter_context(self.nc.psum_tensor(self._uid(name), shape, dtype))

    def build(self):
        nc = self.nc
        with ExitStack() as es:
            cx = Ctx(nc, es)
            self.cx = cx
            self.X = self.sb(es, "X", [128, NT, D], F32)
            self.Xr = cx.regs(NT, "X")
            self.UT = self.sb(es, "UT", [128, KC, S], BF16)
            self.UTr = cx.regs(NT, "UT")
            self.C16 = self.sb(es, "C16", [128, 128], BF16)
            self.C32 = self.sb(es, "C32", [128, C32_N], F32)
            self.GBr = cx.reg("GB")
            self.SS = self.sb(es, "SS", [128, NT], F32)
            self.RSTD = self.sb(es, "RSTD", [128, NT], F32)
            self.PJ = [self.sb(es, "pjunk%d" % j, [128, D], BF16) for j in range(2)]
            self.PJr = cx.regs(2, "pjunk")
            self.SSall = cx.reg("SS")
            self.RSall = cx.reg("RSTD")
            self.cr = cx.reg("consts")
            ds_c = cx.dsem("ds_const")
            cx.dma(cx.pool, ds_c, lambda: nc.gpsimd.dma_start(out=self.C16[:], in_=self.c16_d[:, C16_IDENT:C16_IDENT + 128]), writes=[self.cr])
            cx.dma(cx.sp, ds_c, lambda: nc.sync.dma_start(out=self.C32[:], in_=self.c32_d), writes=[self.cr])
            self.ident = self.C16[:, 0:128]
            ds_x = [cx.dsem("ds_x%d" % i) for i in range(4)]
            xin = self.x_in.rearrange("(i p) d -> p i d", p=128)
            for i in range(NT):
                cx.dma(cx.sp, ds_x[i % 4], (lambda i=i: nc.sync.dma_start(out=self.X[:, i, :], in_=xin[:, i, :])),
                       writes=[self.Xr[i]])
            for ph in self.phases:
                l = int(ph[-1])
                if ph.startswith("mix"):
                    if l == 0:
                        self.norm_transpose(self.mix_g[l:l + 1, :])
                        cx.barrier()
                        self.mlstm_mixer()
                    else:
                        self.pool_mixer(self.mix_g[l:l + 1, :])
                else:
                    self.mlp(l, self.mlp_g[l:l + 1, :], fuse_final=(self.final and ph is self.phases[-1]))
                if ph is not self.phases[-1]:
                    self.rstd_all(self.PJ, self.PJr)
                    self._stats_ready = True
                cx.barrier()
            self.final_out()
        return nc

    def load_gain(self, g_row):
        nc, cx = self.nc, self.cx
        if not hasattr(self, "ds_g"):
            self.ds_g = cx.dsem("ds_g")
        cx.dma(cx.sp, self.ds_g, lambda: nc.sync.dma_start(out=self.GB[:], in_=g_row[0, :].partition_broadcast(128)),
               writes=[self.GBr])

    def rstd_all(self, junk, junk_r):
        nc, cx = self.nc, self.cx
        if getattr(self, "_stats_ready", False):
            self._stats_ready = False
            return
        nj = len(junk)
        for i in range(NT):
            b = i % nj
            cx.op(cx.act, lambda i=i, b=b: nc.scalar.activation(out=junk[b][:], in_=self.X[:, i, :], func=AF.Square,
                                                              accum_out=self.SS[:, i:i + 1]),
                  reads=[self.Xr[i]], writes=[junk_r[b], self.SSall])
        cx.op(cx.dve, lambda: nc.vector.tensor_scalar(out=self.RSTD[:], in0=self.SS[:], scalar1=1.0 / D, scalar2=EPS,
                                                      op0=ALU.mult, op1=ALU.add), reads=[self.SSall], writes=[self.RSall])
        cx.op(cx.act, lambda: nc.scalar.activation(out=self.RSTD[:], in_=self.RSTD[:], func=AF.Sqrt),
              reads=[self.RSall], writes=[self.RSall])
        cx.op(cx.dve, lambda: nc.vector.reciprocal(out=self.RSTD[:], in_=self.RSTD[:]), reads=[self.RSall], writes=[self.RSall])

    def norm_transpose(self, g_row):
        nc, cx = self.nc, self.cx
        with ExitStack() as es:
            self.GB = self.sb(es, "GB", [128, D], F32)
            self.load_gain(g_row)
            junk = [self.sb(es, "nt_junk%d" % j, [128, D], BF16) for j in range(2)]
            junk_r = cx.regs(2, "junk")
            NB = 3
            un = [self.sb(es, "nt_un%d" % j, [128, D], BF16) for j in range(NB)]
            un_r = cx.regs(NB, "un")
            tp = [self.ps(es, "nt_tp%d" % j, [128, KC, 128], BF16) for j in range(NB)]
            tp_r = cx.regs(NB, "tp")
            self.rstd_all(junk, junk_r)
            for i in range(NT):
                b = i % NB
                cx.op(cx.dve, lambda: nc.vector.scalar_tensor_tensor(
                    out=un[b][:], in0=self.X[:, i, :], scalar=self.RSTD[:, i:i + 1], in1=self.GB[:],
                    op0=ALU.mult, op1=ALU.mult),
                    reads=[self.Xr[i], self.RSall, self.GBr], writes=[un_r[b]])
                for k in range(KC):
                    cx.op(cx.pe, lambda k=k: nc.tensor.transpose(tp[b][:, k, :], un[b][:, k * 128:(k + 1) * 128],
                                                                 self.ident),
                          reads=[un_r[b], self.cr], writes=[tp_r[b]], inc=(k == KC - 1))
                cx.op(cx.act, lambda: nc.scalar.copy(out=self.UT[:, :, i * 128:(i + 1) * 128], in_=tp[b][:]),
                      reads=[tp_r[b]], writes=[self.UTr[i]])

    def mlp(self, l, g_row, fuse_final=False):
        nc, cx = self.nc, self.cx
        FG = 4
        NG = DFF // (128 * FG)
        w1v = self.w1[l].rearrange("(k p) f -> p k f", p=128)
        w2v = self.w2[l].rearrange("(c p) n -> p c n", p=128)
        with ExitStack() as es:
            W1 = [self.sb(es, "W1_%d" % j, [128, KC, FG * 128], BF16) for j in range(2)]
            W2 = [self.sb(es, "W2_%d" % j, [128, FG, D], BF16) for j in range(2)]
            W1r, W2r = cx.regs(2, "W1"), cx.regs(2, "W2")
            dW1 = [cx.dsem("ds_w1_%d_%d" % (l, j)) for j in range(2)]
            dW2 = [cx.dsem("ds_w2_%d_%d" % (l, j)) for j in range(2)]

            def load(g):
                b = g % 2
                cx.dma(cx.pool, dW1[b], lambda: nc.gpsimd.dma_start(out=W1[b][:], in_=w1v[:, :, g * FG * 128:(g + 1) * FG * 128]),
                       writes=[W1r[b]])
                cx.dma(cx.pool, dW2[b], lambda: nc.gpsimd.dma_start(out=W2[b][:], in_=w2v[:, g * FG:(g + 1) * FG, :]),
                       writes=[W2r[b]])
            load(0)
            load(1)
            self.norm_transpose(g_row)
            cx.barrier()
            HT = [self.sb(es, "HT%d" % j, [128, FG, S], BF16) for j in range(2)]
            HTr = [[cx.reg("HT") for _ in range(FG * 4)] for _ in range(2)]
            RL = [self.sb(es, "RL%d" % j, [128, 512], F32) for j in range(3)]
            RLr = cx.regs(3, "RL")
            P1 = [self.ps(es, "mp1_%d" % j, [128, 512], F32) for j in range(3)]
            P1r = cx.regs(3, "P1")
            P2 = [self.ps(es, "mp2_%d" % j, [128, 512], F32) for j in range(4)]
            P2r = cx.regs(4, "P2")
            fbufs = self.final_alloc(es) if fuse_final else None

            cnt1 = [0]

            def mm1(g):
                b = g % 2
                for fc in range(FG):
                    for tb in range(4):
                        j = cnt1[0] % 3
                        cnt1[0] += 1
                        for k in range(KC):
                            cx.op(cx.pe, lambda k=k: nc.tensor.matmul(
                                P1[j][:], W1[b][:, k, fc * 128:(fc + 1) * 128], self.UT[:, k, tb * 512:(tb + 1) * 512],
                                start=(k == 0), stop=(k == KC - 1)),
                                reads=[W1r[b]] + self.UTr[tb * 4:tb * 4 + 4], writes=[P1r[j]], inc=(k == KC - 1))
                        cx.op(cx.act, lambda: nc.scalar.activation(out=RL[j][:], in_=P1[j][:], func=AF.Relu),
                              reads=[P1r[j]], writes=[RLr[j]])
                        cx.op(cx.pool, lambda: nc.gpsimd.tensor_tensor(
                            out=HT[b][:, fc, tb * 512:(tb + 1) * 512], in0=RL[j][:], in1=RL[j][:], op=ALU.mult),
                            reads=[RLr[j]], writes=[HTr[b][fc * 4 + tb]])

            cnt2 = [0]

            def mm2(g):
                b = g % 2
                for i in range(NT):
                    for nh in range(2):
                        j = cnt2[0] % 4
                        cnt2[0] += 1
                        for fc in range(FG):
                            cx.op(cx.pe, lambda fc=fc: nc.tensor.matmul(
                                P2[j][:], HT[b][:, fc, i * 128:(i + 1) * 128], W2[b][:, fc, nh * 512:(nh + 1) * 512],
                                start=(fc == 0), stop=(fc == FG - 1)),
                                reads=[W2r[b], HTr[b][fc * 4 + i // 4]], writes=[P2r[j]], inc=(fc == FG - 1))
                        cx.op(cx.dve, lambda: nc.vector.tensor_tensor(
                            out=self.X[:, i, nh * 512:(nh + 1) * 512], in0=P2[j][:], in1=self.X[:, i, nh * 512:(nh + 1) * 512],
                            op=ALU.add),
                            reads=[P2r[j], self.Xr[i]], writes=[self.Xr[i]])

            mm1(0)
            for g in range(NG):
                if g + 1 < NG:
                    mm1(g + 1)
                mm2(g)
                if g + 2 < NG:
                    load(g + 2)
            if fuse_final:
                self.final_emit(fbufs)

    def final_alloc(self, es):
        cx = self.cx
        return dict(
            GB=self.sb(es, "GBf", [128, D], F32),
            junk=[self.sb(es, "fo_junk%d" % j, [128, D], BF16) for j in range(2)], junk_r=cx.regs(2, "junk"),
            yo=[self.sb(es, "fo_y%d" % j, [128, D], F32) for j in range(3)], yo_r=cx.regs(3, "yo"),
            ds_y=[cx.dsem("ds_y%d" % j) for j in range(3)])

    def final_emit(self, fb):
        nc, cx = self.nc, self.cx
        yv = self.y_out.rearrange("(i p) d -> p i d", p=128)
        ds_y, yo, yo_r = fb["ds_y"], fb["yo"], fb["yo_r"]
        self.GB = fb["GB"]
        self.load_gain(self.fin_g)
        self.rstd_all(fb["junk"], fb["junk_r"])
        for i in range(NT):
            b = i % 3
            cx.op(cx.dve, lambda: nc.vector.scalar_tensor_tensor(
                out=yo[b][:], in0=self.X[:, i, :], scalar=self.RSTD[:, i:i + 1], in1=self.GB[:],
                op0=ALU.mult, op1=ALU.mult),
                reads=[self.Xr[i], self.RSall, self.GBr], writes=[yo_r[b]])
            cx.dma(cx.sp, ds_y[b], lambda: nc.sync.dma_start(out=yv[:, i, :], in_=yo[b][:]), reads=[yo_r[b]])
        for d in ds_y:
            if d.cnt:
                nc.sync.wait_ge(d.sem, d.cnt)
        self._final_done = True

    def final_out(self):
        nc, cx = self.nc, self.cx
        if getattr(self, "_final_done", False):
            return
        yv = self.y_out.rearrange("(i p) d -> p i d", p=128)
        with ExitStack() as es:
            if not self.final:
                ds_y = [cx.dsem("ds_y%d" % j) for j in range(2)]
                for i in range(NT):
                    cx.dma(cx.sp, ds_y[i % 2], lambda i=i: nc.sync.dma_start(out=yv[:, i, :], in_=self.X[:, i, :]),
                           reads=[self.Xr[i]])
                for d in ds_y:
                    if d.cnt:
                        nc.sync.wait_ge(d.sem, d.cnt)
            else:
                self.final_emit(self.final_alloc(es))

    def mlstm_mixer(self):
        nc, cx = self.nc, self.cx
        wv = self.w_in.rearrange("(k p) n -> p k n", p=128)
        woutv = self.w_out.rearrange("(h p) n -> p h n", p=128)
        C32 = self.C32
        TRI2 = C32[:, C32_TRIF:C32_TRIF + 256].rearrange("p (d t) -> p d t", d=2)
        TRI = [C32[:, C32_TRIF:C32_TRIF + 128], C32[:, C32_TRIB:C32_TRIB + 128]]
        MU = [C32[:, C32_MUF:C32_MUF + 128], C32[:, C32_MUB:C32_MUB + 128]]
        NEG = [C32[:, C32_NEGF:C32_NEGF + 128], C32[:, C32_NEGB:C32_NEGB + 128]]
        ONES32 = C32[:, C32_ONES:C32_ONES + 128]
        ID32 = C32[:, C32_IDENT:C32_IDENT + 128]
        with ExitStack() as es:
            sb = lambda name, shape, dt: self.sb(es, "ml_" + name, shape, dt)
            PSB = [self.ps(es, "ml_bank%d" % j, [128, 512], F32) for j in range(8)]
            PSr = cx.regs(8, "bank")
            LI = sb("li", [128, NT, 2, 8], F32)
            LF = sb("lf", [128, NT, 2, 8], F32)
            G = sb("g", [128, NT, 2, 8], F32)
            WKK = sb("wkk", [128, NT, 2, 8], F32)
            GX = sb("gx", [128, NT, 2, 8], F32)
            GXP = sb("gxp", [128, NT, 2], F32)
            gr = cx.reg("gates")
            gxpr = cx.reg("gxp")
            WQK = sb("wqk", [128, KC, 256], BF16)
            WV = sb("wv", [128, KC, 256], BF16)
            WO = sb("wo", [128, KC, 256], BF16)
            WOUT = sb("wout", [128, 2, D], BF16)
            HGB = sb("hgb", [128, 256], F32)
            wr1, wr2 = cx.reg("mlw1"), cx.reg("mlw2")
            dsw, dsw1, dsw2 = cx.dsem("ds_mlw"), cx.dsem("ds_mlw1"), cx.dsem("ds_mlw2")
            QTbd = sb("qt", [128, 2, S], BF16)
            KT2 = sb("kt", [128, S], BF16)
            KTOK = sb("ktok", [128, NT, 128], BF16)
            VA = sb("va", [128, NT, 2, 129], BF16)
            H32 = sb("h32", [128, NT, 256], F32)
            PTt = [sb("pt%d" % j, [128, 2, 2, 128], BF16) for j in range(2)]
            CBall = [sb("cball%d" % d, [128, NT, 129], BF16) for d in range(2)]
            KWall = sb("kwall", [128, NT, 2, 128], BF16)
            QTr, KTr = cx.regs(4, "QT"), cx.regs(4, "KT")
            KTOKr, Vr, Hr, PTr = cx.regs(NT), cx.regs(NT), cx.regs(NT), cx.regs(2)
            CBAr = [cx.regs(NT) for _ in range(2)]
            KWr = [cx.regs(NT) for _ in range(2)]
            var1 = cx.reg("va_ones")
            LFM = [sb("lfm%d" % j, [128, 2, 2, 128], F32) for j in range(2)]
            AT = [sb("at%d" % j, [128, 2, 2, 128], F32) for j in range(2)]
            LFMr, ATr = cx.regs(2), cx.regs(2)
            NEG4 = sb("neg4", [128, 2, 2, 128], BF16)
            negr = cx.reg()
            es_g = ExitStack()
            WGt = self.sb(es_g, "ml_wg", [128, KC, 32], BF16)
            GBI = self.sb(es_g, "ml_gbi", [128, 32], F32)
            TH = self.sb(es_g, "ml_th", [128, NT, 32], F32)
            cx.dma(cx.pool, dsw, lambda: nc.gpsimd.dma_start(out=WGt[:], in_=wv[:, :, 3072:3104]), writes=[gr])
            cx.dma(cx.sp, dsw, lambda: nc.sync.dma_start(out=GBI[:], in_=self.gate_b[0, :].partition_broadcast(128)), writes=[gr])
            for d in range(2):
                cx.op(cx.pool, lambda d=d: nc.gpsimd.tensor_copy(out=NEG4[:, d], in_=NEG[d].unsqueeze(1).broadcast_to([128, 2, 128])),
                      reads=[self.cr], writes=[negr])
            cx.op(cx.pool, lambda: nc.gpsimd.memset(VA[:, :, :, 128:129], 1.0), writes=[var1])
            cx.op(cx.pool, lambda: nc.gpsimd.memset(QTbd[:], 0.0), writes=QTr)
            cx.op(cx.pool, lambda: nc.gpsimd.memset(CBall[0][:, 0, :], 0.0), writes=[CBAr[0][0]])
            cx.op(cx.pool, lambda: nc.gpsimd.memset(CBall[1][:, NT - 1, :], 0.0), writes=[CBAr[1][NT - 1]])

            def load_w1(p):
                h0 = 2 * p
                cx.dma(cx.pool, dsw1, lambda: nc.gpsimd.dma_start(out=WQK[:, :, 0:128], in_=wv[:, :, h0 * 64:h0 * 64 + 128]), writes=[wr1])
                cx.dma(cx.pool, dsw1, lambda: nc.gpsimd.dma_start(out=WQK[:, :, 128:256], in_=wv[:, :, 512 + h0 * 64:512 + h0 * 64 + 128]),
                       writes=[wr1])
                cx.dma(cx.pool, dsw1, lambda: nc.gpsimd.dma_start(out=WV[:], in_=wv[:, :, 1024 + h0 * 128:1024 + h0 * 128 + 256]), writes=[wr1])

            def load_w2(p):
                h0 = 2 * p
                cx.dma(cx.pool, dsw2, lambda: nc.gpsimd.dma_start(out=WO[:], in_=wv[:, :, 2048 + h0 * 128:2048 + h0 * 128 + 256]), writes=[wr2])
                cx.dma(cx.pool, dsw2, lambda: nc.gpsimd.dma_start(out=WOUT[:], in_=woutv[:, h0:h0 + 2, :]), writes=[wr2])
                cx.dma(cx.sp, dsw2, lambda: nc.sync.dma_start(out=HGB[:], in_=self.head_g[0, h0 * 128:h0 * 128 + 256].partition_broadcast(128)),
                       writes=[wr2])
            load_w1(0)
            load_w2(0)

            PG = PSB[0][:].rearrange("p (i g) -> p i g", g=32)
            for i in range(NT):
                for k in range(KC):
                    cx.op(cx.pe, lambda i=i, k=k: nc.tensor.matmul(PG[:, i, :], self.UT[:, k, i * 128:(i + 1) * 128], WGt[:, k, :],
                                                                   start=(k == 0), stop=(k == KC - 1)),
                          reads=[gr, self.UTr[i]], writes=[PSr[0]], inc=(i == NT - 1 and k == KC - 1))
            cx.op(cx.dve, lambda: nc.vector.tensor_tensor(out=TH[:], in0=PG, in1=GBI[:].unsqueeze(1).broadcast_to([128, NT, 32]),
                                                          op=ALU.add), writes=[PSr[0], gr])
            cx.op(cx.act, lambda: nc.scalar.activation(out=TH[:], in_=TH[:], func=AF.Tanh, scale=1.0 / SOFTCAP), writes=[gr])
            THv = TH[:].rearrange("p i (g h) -> p i g h", g=4)
            for d in range(2):
                cx.op(cx.dve, lambda d=d: nc.vector.tensor_scalar(out=LI[:, :, d, :], in0=THv[:, :, 2 * d, :], scalar1=SOFTCAP, scalar2=None,
                                                                  op0=ALU.mult), writes=[gr])
                cx.op(cx.act, lambda d=d: nc.scalar.activation(out=LF[:, :, d, :], in_=THv[:, :, 2 * d + 1, :], func=AF.Exp, scale=-SOFTCAP),
                      writes=[gr])
            cx.op(cx.dve, lambda: nc.vector.tensor_scalar(out=LF[:], in0=LF[:], scalar1=1.0, scalar2=None, op0=ALU.add), writes=[gr])
            cx.op(cx.act, lambda: nc.scalar.activation(out=LF[:], in_=LF[:], func=AF.Ln), writes=[gr])
            cx.op(cx.dve, lambda: nc.vector.tensor_scalar(out=LF[:], in0=LF[:], scalar1=-1.0, scalar2=None, op0=ALU.mult), writes=[gr])
            v3 = lambda ap: ap.rearrange("p (i h) -> p i h", h=8)
            for d in range(2):
                for n, lh in enumerate((TRI[d], MU[d], ONES32)):
                    cx.op(cx.pe, lambda lh=lh, n=n, d=d: nc.tensor.matmul(PSB[1 + d][:, n * 128:(n + 1) * 128], lh, LF[:, :, d, :],
                                                                          start=True, stop=True),
                          reads=[gr, self.cr], writes=[PSr[1 + d]], inc=(n == 2))
                cx.op(cx.act, lambda d=d: nc.scalar.activation(out=G[:, :, d, :], in_=v3(PSB[1 + d][:, 0:128]), func=AF.Exp),
                      writes=[PSr[1 + d], gr])
                cx.op(cx.dve, lambda d=d: nc.vector.tensor_tensor(out=WKK[:, :, d, :], in0=v3(PSB[1 + d][:, 128:256]), in1=LI[:, :, d, :],
                                                                  op=ALU.add), writes=[PSr[1 + d], gr])
                cx.op(cx.act, lambda d=d: nc.scalar.activation(out=WKK[:, :, d, :], in_=WKK[:, :, d, :], func=AF.Exp), writes=[gr])
                cx.op(cx.act, lambda d=d: nc.scalar.activation(out=GX[:, :, d, :], in_=v3(PSB[1 + d][:, 256:384]), func=AF.Exp),
                      writes=[PSr[1 + d], gr])

            es_g.close()
            cx.barrier()
            TMP = [sb("tmp%d" % d, [128, 2, 129], F32) for d in range(2)]
            NUMt = sb("numt", [128, 2, 2, 129], F32)
            NUM = [NUMt[:, 0], NUMt[:, 1]]
            DSt = sb("dst", [128, 8], F32)
            CS = [sb("cs%d" % d, [128, 129], F32) for d in range(2)]
            TMPr, NUMr, DSr, CSr = [cx.regs(2) for _ in range(4)]
            JK = [sb("jk%d" % j, [128, 128], BF16) for j in range(2)]
            HSt = sb("hst_stat", [128, NT, 2], F32)
            HR = sb("hr", [128, NT, 2], F32)
            SG = [sb("sg%d" % j, [128, 256], F32) for j in range(2)]
            HSB = [sb("hsb%d" % j, [128, 256], BF16) for j in range(2)]
            HST = [sb("hstt%d" % j, [128, 2, 128], BF16) for j in range(2)]
            JKr, SGr, HSBr, HSTr = cx.regs(2), cx.regs(2), cx.regs(2), cx.regs(2)
            hsr = cx.reg("hstat")

            import os as _os
            _stop = _os.environ.get("MLSTM_STOP", "all")
            for p in range(int(_os.environ.get("MLSTM_NP", "4")) if _stop != "gates" else 0):
                h0 = 2 * p
                hsl = slice(h0, h0 + 2)
                cnt = [0]

                def nxt():
                    j = cnt[0] % 4
                    cnt[0] += 1
                    return j
                def kv_tile(i):
                    j = nxt()
                    for k in range(KC):
                        cx.op(cx.pe, lambda k=k: nc.tensor.matmul(PSB[j][:, 256:384], self.UT[:, k, i * 128:(i + 1) * 128], WQK[:, k, 128:256],
                                                                  start=(k == 0), stop=(k == KC - 1)),
                              reads=[wr1, self.UTr[i]], writes=[PSr[j]], inc=False)
                    for k in range(KC):
                        cx.op(cx.pe, lambda k=k: nc.tensor.matmul(PSB[j][:, 0:256], self.UT[:, k, i * 128:(i + 1) * 128], WV[:, k, :],
                                                                  start=(k == 0), stop=(k == KC - 1)),
                              reads=[wr1, self.UTr[i]], writes=[PSr[j]], inc=(k == KC - 1))
                    cx.op(cx.act, lambda: nc.scalar.copy(out=KTOK[:, i, :], in_=PSB[j][:, 256:384]), writes=[PSr[j], KTOKr[i]])
                    cx.op(cx.dve, lambda: nc.vector.tensor_copy(out=VA[:, i, :, 0:128], in_=PSB[j][:, 0:256].rearrange("p (h v) -> p h v", h=2)),
                          writes=[PSr[j], Vr[i]])
                def kw_tile(i):
                    for d in range(2):
                        cx.op(cx.dve, lambda d=d: nc.vector.tensor_tensor(
                            out=KWall[:, i, d, :].rearrange("p (h k) -> p h k", h=2), in0=KTOK[:, i, :].rearrange("p (h k) -> p h k", h=2),
                            in1=WKK[:, i, d, hsl].unsqueeze(2).broadcast_to([128, 2, 64]), op=ALU.mult),
                            reads=[KTOKr[i], gr], writes=[KWr[d][i]])
                for hp in range(2):
                    cx.op(cx.dve, lambda hp=hp: nc.vector.tensor_copy(out=GXP[hp * 64:(hp + 1) * 64, :, :], in_=GX[hp * 64:(hp + 1) * 64, :, :, h0 + hp]),
                          reads=[gr], writes=[gxpr])
                def qk_group(which, tb):
                        j = nxt()
                        for k in range(KC):
                            cx.op(cx.pe, lambda k=k: nc.tensor.matmul(PSB[j][:], WQK[:, k, which * 128:(which + 1) * 128],
                                                                      self.UT[:, k, tb * 512:(tb + 1) * 512], start=(k == 0), stop=(k == KC - 1)),
                                  reads=[wr1] + self.UTr[tb * 4:tb * 4 + 4], writes=[PSr[j]], inc=(k == KC - 1))
                        if which == 0:
                            for hp in range(2):
                                cx.op(cx.act, lambda hp=hp: nc.scalar.mul(out=QTbd[hp * 64:(hp + 1) * 64, hp, tb * 512:(tb + 1) * 512],
                                                                          in_=PSB[j][hp * 64:(hp + 1) * 64, :], mul=DK ** -0.5),
                                      writes=[PSr[j], QTr[tb]])
                        else:
                            cx.op(cx.dve, lambda: nc.vector.tensor_copy(out=KT2[:, tb * 512:(tb + 1) * 512], in_=PSB[j][:]),
                                  writes=[PSr[j], KTr[tb]])
                DB = [6, 7]
                for d in range(2):
                    cx.op(cx.dve, lambda d=d: nc.vector.memset(CS[d][:], 0.0), writes=[CSr[d]])

                def emit_dc(stp, d):
                    c = stp if d == 0 else NT - 1 - stp
                    off = (stp % 2) * 256
                    for hp in range(2):
                        cx.op(cx.pe, lambda hp=hp: nc.tensor.matmul(PSB[DB[d]][hp * 64:(hp + 1) * 64, off:off + 129],
                                                                    KWall[:, c, d, hp * 64:(hp + 1) * 64], VA[:, c, hp, :], start=True, stop=True),
                              reads=[KWr[d][c], Vr[c], var1], writes=[PSr[DB[d]]], inc=(hp == 1))
                def s_step(stp):
                    for d in range(2):
                        c = stp if d == 0 else NT - 1 - stp
                        cn = c + 1 if d == 0 else c - 1
                        off = (stp % 2) * 256
                        if stp + 1 < NT - 1:
                            emit_dc(stp + 1, d)
                        cx.op(cx.dve, lambda: nc.vector.scalar_tensor_tensor(out=CS[d][:], in0=CS[d][:], scalar=GXP[:, c, d:d + 1],
                                                                             in1=PSB[DB[d]][:, off:off + 129], op0=ALU.mult, op1=ALU.add),
                              reads=[gxpr], writes=[PSr[DB[d]], CSr[d]])
                        cx.op(cx.act, lambda: nc.scalar.copy(out=CBall[d][:, cn, :], in_=CS[d][:]), reads=[CSr[d]], writes=[CBAr[d][cn]])
                for s8 in range(8):
                    for i in (s8, NT - 1 - s8):
                        kv_tile(i)
                        kw_tile(i)
                    if s8 == 0:
                        for d in range(2):
                            emit_dc(0, d)
                    if s8 >= 1:
                        s_step(s8 - 1)
                s_step(7)
                for g8 in range(8):
                    qk_group(g8 // 4, g8 % 4)
                    if 8 + g8 < NT - 1:
                        s_step(8 + g8)
                IB = [2, 4]
                NB_ = [3, 5]
                def emitA(c):
                    a = c % 2
                    tok = slice(c * 128, (c + 1) * 128)
                    tbk = c // 4
                    eb, sbk = a, 6 + a
                    cx.op(cx.pool, lambda: nc.gpsimd.tensor_tensor(out=LFM[a][:], in0=TRI2.unsqueeze(2).broadcast_to([128, 2, 2, 128]),
                                                                   in1=LF[:, c, :, hsl].unsqueeze(3).broadcast_to([128, 2, 2, 128]), op=ALU.mult),
                          reads=[gr, self.cr], writes=[LFMr[a]])
                    for d in range(2):
                        cx.op(cx.pe, lambda d=d: nc.tensor.matmul(PSB[eb][:, d * 256:(d + 1) * 256], MU[d], LFM[a][:, d], start=True, stop=False),
                              reads=[LFMr[a], self.cr], writes=[PSr[eb]], inc=False)
                        cx.op(cx.pe, lambda d=d: nc.tensor.matmul(PSB[eb][:, d * 256:(d + 1) * 256], self.ident, NEG4[:, d], start=False, stop=True),
                              reads=[negr, self.cr], writes=[PSr[eb]], inc=(d == 1))
                    cx.op(cx.pe, lambda: nc.tensor.matmul(PSB[sbk][:, 0:256], KT2[:, tok], QTbd[:, :, tok], start=True, stop=True),
                          reads=[KTr[tbk], QTr[tbk]], writes=[PSr[sbk]])
                    E4 = PSB[eb][:].rearrange("p (d h t) -> p d h t", d=2, h=2)
                    for d in range(2):
                        for hp in range(2):
                            cx.op(cx.act, lambda d=d, hp=hp: nc.scalar.activation(out=AT[a][:, d, hp, :], in_=E4[:, d, hp, :], func=AF.Exp,
                                                                                 bias=LI[:, c, d, h0 + hp:h0 + hp + 1]),
                                  reads=[gr], writes=[PSr[eb], ATr[a]])
                    cx.op(cx.dve, lambda: nc.vector.tensor_tensor(
                        out=PTt[a][:], in0=PSB[sbk][:, 0:256].rearrange("p (h t) -> p h t", h=2).unsqueeze(1).broadcast_to([128, 2, 2, 128]),
                        in1=AT[a][:], op=ALU.mult),
                        reads=[ATr[a]], writes=[PSr[sbk], PTr[a]])

                def emitB(c):
                    a = c % 2
                    tok = slice(c * 128, (c + 1) * 128)
                    tbk = c // 4
                    eb, sbk = a, 6 + a
                    for d in range(2):
                        for hp in range(2):
                            cx.op(cx.pe, lambda hp=hp: nc.tensor.matmul(PSB[IB[d]][:, hp * 256:hp * 256 + 129], PTt[a][:, d, hp, :], VA[:, c, hp, :],
                                                                        start=True, stop=True),
                                  reads=[PTr[a], Vr[c], var1], writes=[PSr[IB[d]]], inc=(hp == 1))
                        for hp in range(2):
                            cx.op(cx.pe, lambda hp=hp: nc.tensor.matmul(PSB[NB_[d]][:, hp * 256:hp * 256 + 129], QTbd[:, hp, tok],
                                                                        CBall[d][:, c, :], start=True, stop=True),
                                  reads=[QTr[tbk], CBAr[d][c]], writes=[PSr[NB_[d]]], inc=(hp == 1))
                    for d in range(2):
                        iv = PSB[NB_[d]][:].rearrange("p (h x) -> p h x", h=2)[:, :, 0:129]
                        nv = PSB[IB[d]][:].rearrange("p (h x) -> p h x", h=2)[:, :, 0:129]
                        for hp in range(2):
                            cx.op(cx.act, lambda hp=hp: nc.scalar.mul(out=TMP[d][:, hp, :], in_=iv[:, hp, :],
                                                                      mul=G[:, c, d, h0 + hp:h0 + hp + 1]),
                                  reads=[gr], writes=[PSr[NB_[d]], TMPr[d]])
                        cx.op(cx.dve, lambda: nc.vector.tensor_tensor(out=NUM[d], in0=nv, in1=TMP[d][:], op=ALU.add),
                              reads=[TMPr[d]], writes=[PSr[IB[d]], NUMr[d]])
                    den = NUMt[:, :, :, 128]
                    dv = lambda lo: DSt[:, lo:lo + 4].rearrange("p (d h) -> p d h", d=2)
                    cx.op(cx.dve, lambda: nc.vector.scalar_tensor_tensor(out=dv(0), in0=den, scalar=-1.0, in1=den, op0=ALU.mult, op1=ALU.max),
                          reads=[NUMr[0], NUMr[1]], writes=[DSr[0]])
                    cx.op(cx.dve, lambda: nc.vector.tensor_scalar(out=DSt[:, 0:4], in0=DSt[:, 0:4], scalar1=1.0, scalar2=None, op0=ALU.max),
                          writes=[DSr[0]])
                    cx.op(cx.dve, lambda: nc.vector.reciprocal(out=DSt[:, 4:8], in_=DSt[:, 0:4]), writes=[DSr[0]])
                    h3 = H32[:, c, :].rearrange("p (h v) -> p h v", h=2)
                    cx.op(cx.dve, lambda: nc.vector.tensor_tensor(out=h3, in0=NUMt[:, 0, :, 0:128],
                                                                  in1=DSt[:, 4:6].unsqueeze(2).broadcast_to([128, 2, 128]), op=ALU.mult),
                          reads=[NUMr[0], DSr[0]], writes=[Hr[c]])
                    for hp in range(2):
                        cx.op(cx.dve, lambda hp=hp: nc.vector.scalar_tensor_tensor(
                            out=h3[:, hp, :], in0=NUMt[:, 1, hp, 0:128], scalar=DSt[:, 6 + hp:7 + hp], in1=h3[:, hp, :],
                            op0=ALU.mult, op1=ALU.add),
                            reads=[NUMr[1], DSr[0]], writes=[Hr[c]])
                def stat_tile(i):
                    for hp in range(2):
                        cx.op(cx.act, lambda hp=hp: nc.scalar.activation(out=JK[hp][:], in_=H32[:, i, hp * 128:(hp + 1) * 128], func=AF.Square,
                                                                         accum_out=HSt[:, i, hp:hp + 1]),
                              reads=[Hr[i]], writes=[JKr[hp], hsr])
                emitA(0)
                for c in range(NT):
                    if c + 1 < NT:
                        emitA(c + 1)
                    if c == NT - 2 and p + 1 < 4:
                        load_w1(p + 1)
                    emitB(c)
                for i in range(NT):
                    stat_tile(i)
                cx.op(cx.dve, lambda: nc.vector.tensor_scalar(out=HR[:], in0=HSt[:], scalar1=1.0 / DV, scalar2=EPS, op0=ALU.mult, op1=ALU.add),
                      writes=[hsr])
                cx.op(cx.act, lambda: nc.scalar.activation(out=HR[:], in_=HR[:], func=AF.Sqrt), writes=[hsr])
                cx.op(cx.dve, lambda: nc.vector.reciprocal(out=HR[:], in_=HR[:]), writes=[hsr])
                for g4 in range(4):
                    sl = slice(g4 * 4, g4 * 4 + 4)
                    hv = H32[:, sl, :].rearrange("p c (h v) -> p c h v", h=2)
                    cx.op(cx.dve, lambda: nc.vector.tensor_tensor(out=hv, in0=hv, in1=HR[:, sl, :].unsqueeze(3).broadcast_to([128, 4, 2, 128]),
                                                                  op=ALU.mult), reads=[hsr], writes=Hr[sl])

                def st_o(i):
                    b = i % 2
                    j = i % 2
                    for k in range(KC):
                        cx.op(cx.pe, lambda k=k: nc.tensor.matmul(PSB[j][:, 0:256], self.UT[:, k, i * 128:(i + 1) * 128], WO[:, k, :],
                                                                  start=(k == 0), stop=(k == KC - 1)),
                              reads=[wr2, self.UTr[i]], writes=[PSr[j]], inc=(k == KC - 1))
                    cx.op(cx.act, lambda: nc.scalar.activation(out=SG[b][:], in_=PSB[j][:, 0:256], func=AF.Sigmoid), writes=[PSr[j], SGr[b]])
                    cx.op(cx.pool, lambda: nc.gpsimd.tensor_tensor(out=SG[b][:], in0=SG[b][:], in1=HGB[:], op=ALU.mult),
                          reads=[wr2], writes=[SGr[b]])
                    cx.op(cx.pool, lambda: nc.gpsimd.tensor_tensor(out=HSB[b][:], in0=H32[:, i, :], in1=SG[b][:], op=ALU.mult),
                          reads=[Hr[i], SGr[b]], writes=[HSBr[b]])

                def st_t(i):
                    b = i % 2
                    j = 2 + i % 2
                    for hp in range(2):
                        cx.op(cx.pe, lambda hp=hp: nc.tensor.matmul(PSB[j][:, hp * 128:(hp + 1) * 128], HSB[b][:, hp * 128:(hp + 1) * 128], self.ident,
                                                                    start=True, stop=True),
                              reads=[HSBr[b], self.cr], writes=[PSr[j]], inc=(hp == 1))
                    cx.op(cx.act, lambda: nc.scalar.copy(out=HST[b][:], in_=PSB[j][:, 0:256].rearrange("p (h t) -> p h t", h=2)),
                          writes=[PSr[j], HSTr[b]])

                def st_x(i):
                    b = i % 2
                    for nh in range(2):
                        j = 4 + (i * 2 + nh) % 4
                        for hp in range(2):
                            cx.op(cx.pe, lambda hp=hp: nc.tensor.matmul(PSB[j][:], HST[b][:, hp, :], WOUT[:, hp, nh * 512:(nh + 1) * 512],
                                                                        start=(hp == 0), stop=(hp == 1)),
                                  reads=[HSTr[b], wr2], writes=[PSr[j]], inc=(hp == 1))
                        cx.op(cx.dve, lambda: nc.vector.tensor_tensor(out=self.X[:, i, nh * 512:(nh + 1) * 512], in0=PSB[j][:],
                                                                      in1=self.X[:, i, nh * 512:(nh + 1) * 512], op=ALU.add),
                              writes=[PSr[j], self.Xr[i]])
                for n in range(NT + 2):
                    if n < NT:
                        st_o(n)
                    if 1 <= n < NT + 1:
                        st_t(n - 1)
                    if n >= 2:
                        st_x(n - 2)
                if p + 1 < 4:
                    load_w2(p + 1)

    def pool_mixer(self, g_row):
        nc, cx = self.nc, self.cx
        winv = self.p_win.rearrange("(k p) n -> p k n", p=128)
        woutv = self.p_wout.rearrange("(k p) n -> p k n", p=128)
        wgv = self.p_wg.rearrange("g (c p) n -> p (g c) n", p=128)
        with ExitStack() as es:
            WI = self.sb(es, "pm_wi", [128, KC, D], BF16)
            WG = self.sb(es, "pm_wg", [128, KC, 256], BF16)
            WO = WI
            SCB = self.sb(es, "pm_scb", [128, D], F32)
            PB16 = self.sb(es, "pm_pb16", [128, 20 * 128], BF16)
            pbr = cx.reg("pb16")
            WIr, WGr, SCr = cx.reg(), cx.reg(), cx.reg()
            WOr = WIr
            dsw = [cx.dsem("ds_pm%d" % j) for j in range(4)]
            cx.dma(cx.pool, cx.dsem("ds_pb16"), lambda: nc.gpsimd.dma_start(out=PB16[:], in_=self.c16_d[:, C16_POOL:C16_POOL + 20 * 128]), writes=[pbr])
            cx.dma(cx.pool, dsw[0], lambda: nc.gpsimd.dma_start(out=WI[:], in_=winv), writes=[WIr])
            cx.dma(cx.pool, dsw[1], lambda: nc.gpsimd.dma_start(out=WG[:], in_=wgv), writes=[WGr])
            cx.dma(cx.sp, dsw[3], lambda: nc.sync.dma_start(out=SCB[:], in_=self.p_scale[0, :].partition_broadcast(128)),
                   writes=[SCr])
            self.norm_transpose(g_row)
            cx.barrier()
            A = self.sb(es, "pm_a", [128, NT, D], BF16)
            PT = self.sb(es, "pm_pt", [128, KC, S], BF16)
            TMPs = [self.sb(es, "pm_tmp%d" % j, [128, 512], F32) for j in range(2)]
            Ar = cx.regs(NT, "A")
            PTr = [[cx.reg() for _ in range(4)] for _ in range(KC)]
            TMr = cx.regs(2)
            PS = [self.ps(es, "pm_ps%d" % j, [128, 512], F32) for j in range(4)]
            PSr = cx.regs(4)
            cnt = [0]

            def nxt():
                j = cnt[0] % 4
                cnt[0] += 1
                return j
            for i in range(NT):
                for nh in range(2):
                    j = nxt()
                    for k in range(KC):
                        cx.op(cx.pe, lambda k=k: nc.tensor.matmul(
                            PS[j][:], self.UT[:, k, i * 128:(i + 1) * 128], WI[:, k, nh * 512:(nh + 1) * 512],
                            start=(k == 0), stop=(k == KC - 1)),
                            reads=[WIr, self.UTr[i]], writes=[PSr[j]], inc=(k == KC - 1))
                    cx.op(cx.act, lambda: nc.scalar.copy(out=A[:, i, nh * 512:(nh + 1) * 512], in_=PS[j][:]),
                          reads=[PSr[j]], writes=[Ar[i]])
            cx.dma(cx.pool, dsw[2], lambda: nc.gpsimd.dma_start(out=WO[:], in_=woutv), writes=[WOr])
            for cc in range(KC):
                wi = cc // 2

                def blk(kind):
                    o = (wi * 5 + POOL_KINDS.index(kind)) * 128
                    return PB16[:, o:o + 128]
                for tb in range(4):
                    j = nxt()
                    for ii in range(4):
                        i = tb * 4 + ii
                        terms = []
                        if i > 0:
                            terms.append((i - 1, "sub"))
                        terms.append((i, "first" if i == 0 else ("last" if i == NT - 1 else "diag")))
                        if i < NT - 1:
                            terms.append((i + 1, "sup"))
                        for n, (si, kind) in enumerate(terms):
                            cx.op(cx.pe, lambda si=si, kind=kind, n=n: nc.tensor.matmul(
                                PS[j][:, ii * 128:(ii + 1) * 128], A[:, si, cc * 128:(cc + 1) * 128], blk(kind),
                                start=(n == 0), stop=(n == len(terms) - 1)),
                                reads=[Ar[si], pbr], writes=[PSr[j]], inc=(ii == 3 and n == len(terms) - 1))
                    cx.op(cx.act, lambda: nc.scalar.copy(out=PT[:, cc, tb * 512:(tb + 1) * 512], in_=PS[j][:]),
                          reads=[PSr[j]], writes=[PTr[cc][tb]])
            for g in range(4):
                for dc in range(2):
                    for tb in range(4):
                        j = nxt()
                        for ci in range(2):
                            cx.op(cx.pe, lambda ci=ci: nc.tensor.matmul(
                                PS[j][:], WG[:, g * 2 + ci, dc * 128:(dc + 1) * 128], PT[:, g * 2 + ci, tb * 512:(tb + 1) * 512],
                                start=(ci == 0), stop=(ci == 1)),
                                reads=[WGr, PTr[g * 2 + ci][tb]], writes=[PSr[j]], inc=(ci == 1))
                        cx.op(cx.act, lambda: nc.scalar.copy(out=self.UT[:, g * 2 + dc, tb * 512:(tb + 1) * 512], in_=PS[j][:]),
                              reads=[PSr[j]], writes=self.UTr[tb * 4:tb * 4 + 4])
            for i in range(NT):
                for nh in range(2):
                    j = nxt()
                    b = (i * 2 + nh) % 2
                    for k in range(KC):
                        cx.op(cx.pe, lambda k=k: nc.tensor.matmul(
                            PS[j][:], self.UT[:, k, i * 128:(i + 1) * 128], WO[:, k, nh * 512:(nh + 1) * 512],
                            start=(k == 0), stop=(k == KC - 1)),
                            reads=[WOr, self.UTr[i]], writes=[PSr[j]], inc=(k == KC - 1))
                    cx.op(cx.dve, lambda: nc.vector.tensor_tensor(out=TMPs[b][:], in0=PS[j][:], in1=SCB[:, nh * 512:(nh + 1) * 512],
                                                                  op=ALU.mult),
                          reads=[PSr[j], SCr], writes=[TMr[b]])
                    cx.op(cx.pool, lambda: nc.gpsimd.tensor_tensor(
                        out=self.X[:, i, nh * 512:(nh + 1) * 512], in0=self.X[:, i, nh * 512:(nh + 1) * 512], in1=TMPs[b][:],
                        op=ALU.add),
                        reads=[TMr[b], self.Xr[i]], writes=[self.Xr[i]])


_W_NAMES = ["mix_norm_g", "mlp_norm_g", "mlstm_w_in", "mlstm_gate_b", "mlstm_head_g", "mlstm_w_out", "pool_w_in",
            "pool_w_group", "pool_scale", "pool_w_out", "mlp_w1", "mlp_w2", "final_norm_g"]


def _in_maps(inputs, xs):
    c16, c32 = make_consts()
    base = {
        "mix_norm_g": inputs["mix_norm_g"], "mlp_norm_g": inputs["mlp_norm_g"],
        "mlstm_w_in": inputs["mlstm_w_in"][0], "mlstm_gate_b": inputs["mlstm_gate_b"],
        "mlstm_head_g": inputs["mlstm_head_g"], "mlstm_w_out": inputs["mlstm_w_out"][0],
        "pool_w_in": inputs["pool_w_in"][0], "pool_w_group": inputs["pool_w_group"][0],
        "pool_w_out": inputs["pool_w_out"][0], "pool_scale": inputs["pool_scale"],
        "mlp_w1": inputs["mlp_w1"], "mlp_w2": inputs["mlp_w2"],
        "final_norm_g": inputs["final_norm_g"].reshape(1, D),
        "c16": c16, "c32": c32,
    }
    base = {k: np.ascontiguousarray(v, dtype=np.float32) for k, v in base.items()}
    return [dict(base, x=np.ascontiguousarray(x, dtype=np.float32)) for x in xs]


def run(inputs, xs, phases=None, final=True, trace=False):
    nc = Builder(phases, final).build()
    res = run_bass_kernel_spmd(nc, _in_maps(inputs, xs), core_ids=list(range(len(xs))), trace=trace)
    return [r["y"] for r in res.results], res


def kernel(**inputs):
    x = np.asarray(inputs["x"], dtype=np.float32)
    ys, _ = run(inputs, [x[b] for b in range(x.shape[0])])
    return np.stack(ys, axis=0).astype(np.float32)
```

```python
import numpy as np
from contextlib import ExitStack
import concourse.bass as bass
import concourse.mybir as mybir
from concourse.bass_utils import run_bass_kernel_spmd

F32 = mybir.dt.float32
BF16 = mybir.dt.bfloat16
AF = mybir.ActivationFunctionType
ALU = mybir.AluOpType
AX = mybir.AxisListType

S = 2048
D = 1024
NT = 16
KC = 8
DFF = 4096
EPS = 1e-6
NH = 8
DK = 64
DV = 128
SOFTCAP = 15.0
POOL_W = (2, 4, 8, 16)
NEGBIG = -30000.0


class Region:
    __slots__ = ("name", "w", "r")

    def __init__(self, name):
        self.name = name
        self.w = None
        self.r = {}


class Eng:
    def __init__(self, es, nc, h, name):
        self.h = h
        self.name = name
        self.sem = es.enter_context(nc.semaphore("sem_" + name))
        self.cnt = 0
        self.known = {}
        self.pending = False


class DSem:
    def __init__(self, es, nc, name):
        self.sem = es.enter_context(nc.semaphore(name))
        self.cnt = 0


class Ctx:
    def __init__(self, nc, es):
        self.nc = nc
        self.es = es
        self.pe = Eng(es, nc, nc.tensor, "pe")
        self.act = Eng(es, nc, nc.scalar, "act")
        self.dve = Eng(es, nc, nc.vector, "dve")
        self.pool = Eng(es, nc, nc.gpsimd, "pool")
        self.sp = Eng(es, nc, nc.sync, "sp")
        self.engs = [self.pe, self.act, self.dve, self.pool, self.sp]
        self.dsems = []
        self.nreg = 0

    def reg(self, name="r"):
        self.nreg += 1
        return Region(name)

    def regs(self, n, name="r"):
        return [self.reg(name) for _ in range(n)]

    def dsem(self, name):
        self._nd = getattr(self, "_nd", 0) + 1
        d = DSem(self.es, self.nc, "%s_u%d" % (name, self._nd))
        self.dsems.append(d)
        return d

    def _wait(self, eng, deps):
        best = {}
        for sem, val in deps:
            k = id(sem)
            if k not in best or best[k][1] < val:
                best[k] = (sem, val)
        for k, (sem, val) in best.items():
            if sem is eng.sem and val > eng.cnt:
                continue
            if eng.known.get(k, 0) < val:
                eng.h.wait_ge(sem, val)
                eng.known[k] = val

    def _deps(self, reads, writes):
        deps = []
        for r in reads:
            if r.w is not None:
                deps.append(r.w)
        for w in writes:
            if w.w is not None:
                deps.append(w.w)
            deps.extend(w.r.values())
        return deps

    def _update(self, tag, reads, writes):
        k = id(tag[0])
        for r in reads:
            if k not in r.r or r.r[k][1] < tag[1]:
                r.r[k] = tag
        for w in writes:
            w.w = tag
            w.r = {}

    def op(self, eng, fn, reads=(), writes=(), inc=True):
        self._wait(eng, self._deps(reads, writes))
        ins = fn()
        if inc:
            eng.cnt += 1
            ins.then_inc(eng.sem, 1)
            eng.pending = False
            tag = (eng.sem, eng.cnt)
        else:
            eng.pending = True
            tag = (eng.sem, eng.cnt + 1)
        self._update(tag, reads, writes)
        return ins

    def dma(self, eng, dsem, fn, reads=(), writes=()):
        self._wait(eng, self._deps(reads, writes))
        ins = fn()
        dsem.cnt += 16
        ins.then_inc(dsem.sem, 16)
        self._update((dsem.sem, dsem.cnt), reads, writes)
        return ins

    def barrier(self):
        tags = [(e.sem, e.cnt) for e in self.engs if e.cnt > 0]
        tags += [(d.sem, d.cnt) for d in self.dsems if d.cnt > 0]
        for e in self.engs:
            assert not e.pending
            self._wait(e, tags)


def pool_band_blocks():
    out = {}
    t = np.arange(S)
    for w in POOL_W:
        lo = np.clip(t - w // 2, 0, S)
        hi = np.clip(t + w - w // 2, 0, S)
        cnt = (hi - lo).astype(np.float64)

        def block(si, ti):
            blk = np.zeros((128, 128), np.float64)
            for tt in range(128):
                tg = ti * 128 + tt
                for sg in range(lo[tg], hi[tg]):
                    sl = sg - si * 128
                    if 0 <= sl < 128:
                        blk[sl, tt] += 1.0 / cnt[tg]
                if si == ti:
                    blk[tt, tt] -= 1.0
            return blk.astype(np.float32)

        out[w] = {
            "first": block(0, 0), "diag": block(5, 5), "last": block(15, 15),
            "sub": block(4, 5),
            "sup": block(6, 5),
        }
    return out


C16_IDENT = 0
C16_POOL = 128
C16_NEGF = C16_POOL + 20 * 128
C16_NEGB = C16_NEGF + 128
C16_ONES = C16_NEGB + 128
C16_N = C16_ONES + 128
POOL_KINDS = ("first", "diag", "last", "sub", "sup")

C32_TRIF = 0
C32_TRIB = 128
C32_MUF = 256
C32_MUB = 384
C32_ONES = 512
C32_IDENT = 640
C32_NEGF = 768
C32_NEGB = 896
C32_N = 1024


def make_consts():
    c16 = np.zeros((128, C16_N), np.float32)
    c16[:, C16_IDENT:C16_IDENT + 128] = np.eye(128, dtype=np.float32)
    pb = pool_band_blocks()
    for wi, w in enumerate(POOL_W):
        for ki, kind in enumerate(POOL_KINDS):
            o = C16_POOL + (wi * 5 + ki) * 128
            c16[:, o:o + 128] = pb[w][kind]
    r = np.arange(128)[:, None]
    t = np.arange(128)[None, :]
    c16[:, C16_NEGF:C16_NEGF + 128] = NEGBIG * (r > t)
    c16[:, C16_NEGB:C16_NEGB + 128] = NEGBIG * (r < t)
    c16[:, C16_ONES:C16_ONES + 128] = 1.0
    c32 = np.zeros((128, C32_N), np.float32)
    c32[:, C32_TRIF:C32_TRIF + 128] = (r <= t)
    c32[:, C32_TRIB:C32_TRIB + 128] = (r >= t)
    c32[:, C32_MUF:C32_MUF + 128] = (r > t)
    c32[:, C32_MUB:C32_MUB + 128] = (r < t)
    c32[:, C32_ONES:C32_ONES + 128] = 1.0
    c32[:, C32_IDENT:C32_IDENT + 128] = np.eye(128, dtype=np.float32)
    c32[:, C32_NEGF:C32_NEGF + 128] = NEGBIG * (r > t)
    c32[:, C32_NEGB:C32_NEGB + 128] = NEGBIG * (r < t)
    return c16, c32


class Builder:
    def __init__(self, phases=None, final=True):
        self.phases = phases if phases is not None else ["mix0", "mlp0", "mix1", "mlp1"]
        self.final = final
        nc = bass.Bass("TRN2", target_bir_lowering=False)
        self.nc = nc
        dt = nc.dram_tensor
        self.x_in = dt("x", [S, D], F32, kind="ExternalInput").ap()
        self.mix_g = dt("mix_norm_g", [2, D], F32, kind="ExternalInput").ap()
        self.mlp_g = dt("mlp_norm_g", [2, D], F32, kind="ExternalInput").ap()
        self.w_in = dt("mlstm_w_in", [D, 3104], F32, kind="ExternalInput").ap()
        self.gate_b = dt("mlstm_gate_b", [1, 32], F32, kind="ExternalInput").ap()
        self.head_g = dt("mlstm_head_g", [1, D], F32, kind="ExternalInput").ap()
        self.w_out = dt("mlstm_w_out", [D, D], F32, kind="ExternalInput").ap()
        self.p_win = dt("pool_w_in", [D, D], F32, kind="ExternalInput").ap()
        self.p_wg = dt("pool_w_group", [4, 256, 256], F32, kind="ExternalInput").ap()
        self.p_wout = dt("pool_w_out", [D, D], F32, kind="ExternalInput").ap()
        self.p_scale = dt("pool_scale", [1, D], F32, kind="ExternalInput").ap()
        self.w1 = dt("mlp_w1", [2, D, DFF], F32, kind="ExternalInput").ap()
        self.w2 = dt("mlp_w2", [2, DFF, D], F32, kind="ExternalInput").ap()
        self.fin_g = dt("final_norm_g", [1, D], F32, kind="ExternalInput").ap()
        self.c16_d = dt("c16", [128, C16_N], F32, kind="ExternalInput").ap()
        self.c32_d = dt("c32", [128, C32_N], F32, kind="ExternalInput").ap()
        self.y_out = dt("y", [S, D], F32, kind="ExternalOutput").ap()

    def _uid(self, name):
        self._n = getattr(self, "_n", 0) + 1
        return "%s_u%d" % (name, self._n)

    def sb(self, es, name, shape, dtype):
        return es.enter_context(self.nc.sbuf_tensor(self._uid(name), shape, dtype))

    def ps(self, es, name, shape, dtype=F32):
        return es.enter_context(self.nc.psum_tensor(self._uid(name), shape, dtype))

    def build(self):
        nc = self.nc
        with ExitStack() as es:
            cx = Ctx(nc, es)
            self.cx = cx
            self.X = self.sb(es, "X", [128, NT, D], F32)
            self.Xr = cx.regs(NT, "X")
            self.UT = self.sb(es, "UT", [128, KC, S], BF16)
            self.UTr = cx.regs(NT, "UT")
            self.C16 = self.sb(es, "C16", [128, 128], BF16)
            self.C32 = self.sb(es, "C32", [128, C32_N], F32)
            self.GBr = cx.reg("GB")
            self.SS = self.sb(es, "SS", [128, NT], F32)
            self.RSTD = self.sb(es, "RSTD", [128, NT], F32)
            self.PJ = [self.sb(es, "pjunk%d" % j, [128, D], BF16) for j in range(2)]
            self.PJr = cx.regs(2, "pjunk")
            self.SSall = cx.reg("SS")
            self.RSall = cx.reg("RSTD")
            self.cr = cx.reg("consts")
            ds_c = cx.dsem("ds_const")
            cx.dma(cx.pool, ds_c, lambda: nc.gpsimd.dma_start(out=self.C16[:], in_=self.c16_d[:, C16_IDENT:C16_IDENT + 128]), writes=[self.cr])
            cx.dma(cx.sp, ds_c, lambda: nc.sync.dma_start(out=self.C32[:], in_=self.c32_d), writes=[self.cr])
            self.ident = self.C16[:, 0:128]
            ds_x = [cx.dsem("ds_x%d" % i) for i in range(4)]
            xin = self.x_in.rearrange("(i p) d -> p i d", p=128)
            for i in range(NT):
                cx.dma(cx.sp, ds_x[i % 4], (lambda i=i: nc.sync.dma_start(out=self.X[:, i, :], in_=xin[:, i, :])),
                       writes=[self.Xr[i]])
            for ph in self.phases:
                l = int(ph[-1])
                if ph.startswith("mix"):
                    if l == 0:
                        self.norm_transpose(self.mix_g[l:l + 1, :])
                        cx.barrier()
                        self.mlstm_mixer()
                    else:
                        self.pool_mixer(self.mix_g[l:l + 1, :])
                else:
                    self.mlp(l, self.mlp_g[l:l + 1, :], fuse_final=(self.final and ph is self.phases[-1]))
                if ph is not self.phases[-1]:
                    self.rstd_all(self.PJ, self.PJr)
                    self._stats_ready = True
                cx.barrier()
            self.final_out()
        return nc

    def load_gain(self, g_row):
        nc, cx = self.nc, self.cx
        if not hasattr(self, "ds_g"):
            self.ds_g = cx.dsem("ds_g")
        cx.dma(cx.sp, self.ds_g, lambda: nc.sync.dma_start(out=self.GB[:], in_=g_row[0, :].partition_broadcast(128)),
               writes=[self.GBr])

    def rstd_all(self, junk, junk_r):
        nc, cx = self.nc, self.cx
        if getattr(self, "_stats_ready", False):
            self._stats_ready = False
            return
        nj = len(junk)
        for i in range(NT):
            b = i % nj
            cx.op(cx.act, lambda i=i, b=b: nc.scalar.activation(out=junk[b][:], in_=self.X[:, i, :], func=AF.Square,
                                                              accum_out=self.SS[:, i:i + 1]),
                  reads=[self.Xr[i]], writes=[junk_r[b], self.SSall])
        cx.op(cx.dve, lambda: nc.vector.tensor_scalar(out=self.RSTD[:], in0=self.SS[:], scalar1=1.0 / D, scalar2=EPS,
                                                      op0=ALU.mult, op1=ALU.add), reads=[self.SSall], writes=[self.RSall])
        cx.op(cx.act, lambda: nc.scalar.activation(out=self.RSTD[:], in_=self.RSTD[:], func=AF.Sqrt),
              reads=[self.RSall], writes=[self.RSall])
        cx.op(cx.dve, lambda: nc.vector.reciprocal(out=self.RSTD[:], in_=self.RSTD[:]), reads=[self.RSall], writes=[self.RSall])

    def norm_transpose(self, g_row, NB=3, gb=None):
        nc, cx = self.nc, self.cx
        with ExitStack() as es:
            self.GB = gb if gb is not None else self.sb(es, "GB", [128, D], F32)
            self.load_gain(g_row)
            junk, junk_r = self.PJ, self.PJr
            un = [self.sb(es, "nt_un%d" % j, [128, D], BF16) for j in range(NB)]
            un_r = cx.regs(NB, "un")
            tp = [self.ps(es, "nt_tp%d" % j, [128, KC, 128], BF16) for j in range(NB)]
            tp_r = cx.regs(NB, "tp")
            self.rstd_all(junk, junk_r)
            for i in range(NT):
                b = i % NB
                cx.op(cx.dve, lambda: nc.vector.scalar_tensor_tensor(
                    out=un[b][:], in0=self.X[:, i, :], scalar=self.RSTD[:, i:i + 1], in1=self.GB[:],
                    op0=ALU.mult, op1=ALU.mult),
                    reads=[self.Xr[i], self.RSall, self.GBr], writes=[un_r[b]])
                for k in range(KC):
                    cx.op(cx.pe, lambda k=k: nc.tensor.transpose(tp[b][:, k, :], un[b][:, k * 128:(k + 1) * 128],
                                                                 self.ident),
                          reads=[un_r[b], self.cr], writes=[tp_r[b]], inc=(k == KC - 1))
                cx.op(cx.act, lambda: nc.scalar.copy(out=self.UT[:, :, i * 128:(i + 1) * 128], in_=tp[b][:]),
                      reads=[tp_r[b]], writes=[self.UTr[i]])

    def mlp(self, l, g_row, fuse_final=False):
        nc, cx = self.nc, self.cx
        FG = 4
        NG = DFF // (128 * FG)
        w1v = self.w1[l].rearrange("(k p) f -> p k f", p=128)
        w2v = self.w2[l].rearrange("(c p) n -> p c n", p=128)
        with ExitStack() as es:
            W1 = [self.sb(es, "W1_%d" % j, [128, KC, FG * 128], BF16) for j in range(2)]
            W2 = [self.sb(es, "W2_%d" % j, [128, FG, D], BF16) for j in range(2)]
            W1r, W2r = cx.regs(2, "W1"), cx.regs(2, "W2")
            dW1 = [cx.dsem("ds_w1_%d_%d" % (l, j)) for j in range(2)]
            dW2 = [cx.dsem("ds_w2_%d_%d" % (l, j)) for j in range(2)]

            def load(g):
                b = g % 2
                cx.dma(cx.pool, dW1[b], lambda: nc.gpsimd.dma_start(out=W1[b][:], in_=w1v[:, :, g * FG * 128:(g + 1) * FG * 128]),
                       writes=[W1r[b]])
                cx.dma(cx.pool, dW2[b], lambda: nc.gpsimd.dma_start(out=W2[b][:], in_=w2v[:, g * FG:(g + 1) * FG, :]),
                       writes=[W2r[b]])
            load(0)
            load(1)
            HT = [self.sb(es, "HT%d" % j, [128, FG, S], BF16) for j in range(2)]
            HTr = [[cx.reg("HT") for _ in range(FG * 4)] for _ in range(2)]
            RL = [self.sb(es, "RL%d" % j, [128, 512], F32) for j in range(3)]
            RLr = cx.regs(3, "RL")
            P1 = [self.ps(es, "mp1_%d" % j, [128, 512], F32) for j in range(3)]
            P1r = cx.regs(3, "P1")
            P2 = [self.ps(es, "mp2_%d" % j, [128, 512], F32) for j in range(3)]
            P2r = cx.regs(3, "P2")
            fbufs = self.final_alloc(es) if fuse_final else None
            self.norm_transpose(g_row, NB=2)

            cnt1 = [0]

            def mm1(g):
                b = g % 2
                for fc in range(FG):
                    for tb in range(4):
                        j = cnt1[0] % 3
                        cnt1[0] += 1
                        for k in range(KC):
                            cx.op(cx.pe, lambda k=k: nc.tensor.matmul(
                                P1[j][:], W1[b][:, k, fc * 128:(fc + 1) * 128], self.UT[:, k, tb * 512:(tb + 1) * 512],
                                start=(k == 0), stop=(k == KC - 1)),
                                reads=[W1r[b]] + self.UTr[tb * 4:tb * 4 + 4], writes=[P1r[j]], inc=(k == KC - 1))
                        cx.op(cx.act, lambda: nc.scalar.activation(out=RL[j][:], in_=P1[j][:], func=AF.Relu),
                              reads=[P1r[j]], writes=[RLr[j]])
                        cx.op(cx.pool, lambda: nc.gpsimd.tensor_tensor(
                            out=HT[b][:, fc, tb * 512:(tb + 1) * 512], in0=RL[j][:], in1=RL[j][:], op=ALU.mult),
                            reads=[RLr[j]], writes=[HTr[b][fc * 4 + tb]])

            cnt2 = [0]

            def mm2(g):
                b = g % 2
                for i in range(NT):
                    for nh in range(2):
                        j = cnt2[0] % 3
                        cnt2[0] += 1
                        for fc in range(FG):
                            cx.op(cx.pe, lambda fc=fc: nc.tensor.matmul(
                                P2[j][:], HT[b][:, fc, i * 128:(i + 1) * 128], W2[b][:, fc, nh * 512:(nh + 1) * 512],
                                start=(fc == 0), stop=(fc == FG - 1)),
                                reads=[W2r[b], HTr[b][fc * 4 + i // 4]], writes=[P2r[j]], inc=(fc == FG - 1))
                        cx.op(cx.dve, lambda: nc.vector.tensor_tensor(
                            out=self.X[:, i, nh * 512:(nh + 1) * 512], in0=P2[j][:], in1=self.X[:, i, nh * 512:(nh + 1) * 512],
                            op=ALU.add),
                            reads=[P2r[j], self.Xr[i]], writes=[self.Xr[i]])

            mm1(0)
            for g in range(NG):
                if g + 1 < NG:
                    mm1(g + 1)
                mm2(g)
                if g + 2 < NG:
                    load(g + 2)
            if fuse_final:
                self.final_emit(fbufs)

    def final_alloc(self, es):
        cx = self.cx
        return dict(
            GB=self.sb(es, "GBf", [128, D], F32),
            junk=[self.sb(es, "fo_junk%d" % j, [128, D], BF16) for j in range(2)], junk_r=cx.regs(2, "junk"),
            yo=[self.sb(es, "fo_y%d" % j, [128, D], F32) for j in range(3)], yo_r=cx.regs(3, "yo"),
            ds_y=[cx.dsem("ds_y%d" % j) for j in range(3)])

    def final_emit(self, fb):
        nc, cx = self.nc, self.cx
        yv = self.y_out.rearrange("(i p) d -> p i d", p=128)
        ds_y, yo, yo_r = fb["ds_y"], fb["yo"], fb["yo_r"]
        self.GB = fb["GB"]
        self.load_gain(self.fin_g)
        self.rstd_all(fb["junk"], fb["junk_r"])
        for i in range(NT):
            b = i % 3
            cx.op(cx.dve, lambda: nc.vector.scalar_tensor_tensor(
                out=yo[b][:], in0=self.X[:, i, :], scalar=self.RSTD[:, i:i + 1], in1=self.GB[:],
                op0=ALU.mult, op1=ALU.mult),
                reads=[self.Xr[i], self.RSall, self.GBr], writes=[yo_r[b]])
            cx.dma(cx.sp, ds_y[b], lambda: nc.sync.dma_start(out=yv[:, i, :], in_=yo[b][:]), reads=[yo_r[b]])
        for d in ds_y:
            if d.cnt:
                nc.sync.wait_ge(d.sem, d.cnt)
        self._final_done = True

    def final_out(self):
        nc, cx = self.nc, self.cx
        if getattr(self, "_final_done", False):
            return
        yv = self.y_out.rearrange("(i p) d -> p i d", p=128)
        with ExitStack() as es:
            if not self.final:
                ds_y = [cx.dsem("ds_y%d" % j) for j in range(2)]
                for i in range(NT):
                    cx.dma(cx.sp, ds_y[i % 2], lambda i=i: nc.sync.dma_start(out=yv[:, i, :], in_=self.X[:, i, :]),
                           reads=[self.Xr[i]])
                for d in ds_y:
                    if d.cnt:
                        nc.sync.wait_ge(d.sem, d.cnt)
            else:
                self.final_emit(self.final_alloc(es))

    def mlstm_mixer(self):
        nc, cx = self.nc, self.cx
        wv = self.w_in.rearrange("(k p) n -> p k n", p=128)
        woutv = self.w_out.rearrange("(h p) n -> p h n", p=128)
        C32 = self.C32
        TRI2 = C32[:, C32_TRIF:C32_TRIF + 256].rearrange("p (d t) -> p d t", d=2)
        TRI = [C32[:, C32_TRIF:C32_TRIF + 128], C32[:, C32_TRIB:C32_TRIB + 128]]
        MU = [C32[:, C32_MUF:C32_MUF + 128], C32[:, C32_MUB:C32_MUB + 128]]
        NEG = [C32[:, C32_NEGF:C32_NEGF + 128], C32[:, C32_NEGB:C32_NEGB + 128]]
        ONES32 = C32[:, C32_ONES:C32_ONES + 128]
        ID32 = C32[:, C32_IDENT:C32_IDENT + 128]
        with ExitStack() as es:
            sb = lambda name, shape, dt: self.sb(es, "ml_" + name, shape, dt)
            PSB = [self.ps(es, "ml_bank%d" % j, [128, 512], F32) for j in range(8)]
            PSr = cx.regs(8, "bank")
            LI = sb("li", [128, NT, 2, 8], F32)
            LF = sb("lf", [128, NT, 2, 8], F32)
            G = sb("g", [128, NT, 2, 8], F32)
            WKK = sb("wkk", [128, NT, 2, 8], F32)
            GX = sb("gx", [128, NT, 2, 8], F32)
            GXP = sb("gxp", [128, NT, 2], F32)
            gr = cx.reg("gates")
            gxpr = cx.reg("gxp")
            WQK = sb("wqk", [128, KC, 256], BF16)
            WV = sb("wv", [128, KC, 256], BF16)
            WO = sb("wo", [128, KC, 256], BF16)
            WOUT = sb("wout", [128, 2, D], BF16)
            HGB = sb("hgb", [128, 256], F32)
            wr1, wr2 = cx.reg("mlw1"), cx.reg("mlw2")
            dsw, dsw1, dsw2 = cx.dsem("ds_mlw"), cx.dsem("ds_mlw1"), cx.dsem("ds_mlw2")
            QTbd = sb("qt", [128, 2, S], BF16)
            KT2 = sb("kt", [128, S], BF16)
            KTOK = sb("ktok", [128, NT, 128], BF16)
            VA = sb("va", [128, NT, 2, 129], BF16)
            H32 = sb("h32", [128, NT, 256], F32)
            PTt = [sb("pt%d" % j, [128, 2, 2, 128], BF16) for j in range(2)]
            CBall = [sb("cball%d" % d, [128, NT, 129], BF16) for d in range(2)]
            KWall = sb("kwall", [128, NT, 2, 128], BF16)
            QTr, KTr = cx.regs(4, "QT"), cx.regs(4, "KT")
            KTOKr, Vr, Hr, PTr = cx.regs(NT), cx.regs(NT), cx.regs(NT), cx.regs(2)
            CBAr = [cx.regs(NT) for _ in range(2)]
            KWr = [cx.regs(NT) for _ in range(2)]
            var1 = cx.reg("va_ones")
            LFM = [sb("lfm%d" % j, [128, 2, 2, 128], F32) for j in range(2)]
            AT = [sb("at%d" % j, [128, 2, 2, 128], F32) for j in range(2)]
            LFMr, ATr = cx.regs(2), cx.regs(2)
            NEG4 = sb("neg4", [128, 2, 2, 128], BF16)
            negr = cx.reg()
            es_g = ExitStack()
            WGt = self.sb(es_g, "ml_wg", [128, KC, 32], BF16)
            GBI = self.sb(es_g, "ml_gbi", [128, 32], F32)
            TH = self.sb(es_g, "ml_th", [128, NT, 32], F32)
            cx.dma(cx.pool, dsw, lambda: nc.gpsimd.dma_start(out=WGt[:], in_=wv[:, :, 3072:3104]), writes=[gr])
            cx.dma(cx.sp, dsw, lambda: nc.sync.dma_start(out=GBI[:], in_=self.gate_b[0, :].partition_broadcast(128)), writes=[gr])
            for d in range(2):
                cx.op(cx.pool, lambda d=d: nc.gpsimd.tensor_copy(out=NEG4[:, d], in_=NEG[d].unsqueeze(1).broadcast_to([128, 2, 128])),
                      reads=[self.cr], writes=[negr])
            cx.op(cx.pool, lambda: nc.gpsimd.memset(VA[:, :, :, 128:129], 1.0), writes=[var1])
            cx.op(cx.pool, lambda: nc.gpsimd.memset(QTbd[:], 0.0), writes=QTr)
            cx.op(cx.pool, lambda: nc.gpsimd.memset(CBall[0][:, 0, :], 0.0), writes=[CBAr[0][0]])
            cx.op(cx.pool, lambda: nc.gpsimd.memset(CBall[1][:, NT - 1, :], 0.0), writes=[CBAr[1][NT - 1]])

            def load_w1(p):
                h0 = 2 * p
                cx.dma(cx.pool, dsw1, lambda: nc.gpsimd.dma_start(out=WQK[:, :, 0:128], in_=wv[:, :, h0 * 64:h0 * 64 + 128]), writes=[wr1])
                cx.dma(cx.pool, dsw1, lambda: nc.gpsimd.dma_start(out=WQK[:, :, 128:256], in_=wv[:, :, 512 + h0 * 64:512 + h0 * 64 + 128]),
                       writes=[wr1])
                cx.dma(cx.pool, dsw1, lambda: nc.gpsimd.dma_start(out=WV[:], in_=wv[:, :, 1024 + h0 * 128:1024 + h0 * 128 + 256]), writes=[wr1])

            def load_w2(p):
                h0 = 2 * p
                cx.dma(cx.pool, dsw2, lambda: nc.gpsimd.dma_start(out=WO[:], in_=wv[:, :, 2048 + h0 * 128:2048 + h0 * 128 + 256]), writes=[wr2])
                cx.dma(cx.pool, dsw2, lambda: nc.gpsimd.dma_start(out=WOUT[:], in_=woutv[:, h0:h0 + 2, :]), writes=[wr2])
                cx.dma(cx.sp, dsw2, lambda: nc.sync.dma_start(out=HGB[:], in_=self.head_g[0, h0 * 128:h0 * 128 + 256].partition_broadcast(128)),
                       writes=[wr2])
            load_w1(0)
            load_w2(0)

            PG = PSB[0][:].rearrange("p (i g) -> p i g", g=32)
            for i in range(NT):
                for k in range(KC):
                    cx.op(cx.pe, lambda i=i, k=k: nc.tensor.matmul(PG[:, i, :], self.UT[:, k, i * 128:(i + 1) * 128], WGt[:, k, :],
                                                                   start=(k == 0), stop=(k == KC - 1)),
                          reads=[gr, self.UTr[i]], writes=[PSr[0]], inc=(i == NT - 1 and k == KC - 1))
            cx.op(cx.dve, lambda: nc.vector.tensor_tensor(out=TH[:], in0=PG, in1=GBI[:].unsqueeze(1).broadcast_to([128, NT, 32]),
                                                          op=ALU.add), writes=[PSr[0], gr])
            cx.op(cx.act, lambda: nc.scalar.activation(out=TH[:], in_=TH[:], func=AF.Tanh, scale=1.0 / SOFTCAP), writes=[gr])
            THv = TH[:].rearrange("p i (g h) -> p i g h", g=4)
            for d in range(2):
                cx.op(cx.dve, lambda d=d: nc.vector.tensor_scalar(out=LI[:, :, d, :], in0=THv[:, :, 2 * d, :], scalar1=SOFTCAP, scalar2=None,
                                                                  op0=ALU.mult), writes=[gr])
                cx.op(cx.act, lambda d=d: nc.scalar.activation(out=LF[:, :, d, :], in_=THv[:, :, 2 * d + 1, :], func=AF.Exp, scale=-SOFTCAP),
                      writes=[gr])
            cx.op(cx.dve, lambda: nc.vector.tensor_scalar(out=LF[:], in0=LF[:], scalar1=1.0, scalar2=None, op0=ALU.add), writes=[gr])
            cx.op(cx.act, lambda: nc.scalar.activation(out=LF[:], in_=LF[:], func=AF.Ln), writes=[gr])
            cx.op(cx.dve, lambda: nc.vector.tensor_scalar(out=LF[:], in0=LF[:], scalar1=-1.0, scalar2=None, op0=ALU.mult), writes=[gr])
            v3 = lambda ap: ap.rearrange("p (i h) -> p i h", h=8)
            for d in range(2):
                for n, lh in enumerate((TRI[d], MU[d], ONES32)):
                    cx.op(cx.pe, lambda lh=lh, n=n, d=d: nc.tensor.matmul(PSB[1 + d][:, n * 128:(n + 1) * 128], lh, LF[:, :, d, :],
                                                                          start=True, stop=True),
                          reads=[gr, self.cr], writes=[PSr[1 + d]], inc=(n == 2))
                cx.op(cx.act, lambda d=d: nc.scalar.activation(out=G[:, :, d, :], in_=v3(PSB[1 + d][:, 0:128]), func=AF.Exp),
                      writes=[PSr[1 + d], gr])
                cx.op(cx.dve, lambda d=d: nc.vector.tensor_tensor(out=WKK[:, :, d, :], in0=v3(PSB[1 + d][:, 128:256]), in1=LI[:, :, d, :],
                                                                  op=ALU.add), writes=[PSr[1 + d], gr])
                cx.op(cx.act, lambda d=d: nc.scalar.activation(out=WKK[:, :, d, :], in_=WKK[:, :, d, :], func=AF.Exp), writes=[gr])
                cx.op(cx.act, lambda d=d: nc.scalar.activation(out=GX[:, :, d, :], in_=v3(PSB[1 + d][:, 256:384]), func=AF.Exp),
                      writes=[PSr[1 + d], gr])

            es_g.close()
            cx.barrier()
            TMP = [sb("tmp%d" % d, [128, 2, 129], F32) for d in range(2)]
            NUMt = sb("numt", [128, 2, 2, 129], F32)
            NUM = [NUMt[:, 0], NUMt[:, 1]]
            DSt = sb("dst", [128, 8], F32)
            CS = [sb("cs%d" % d, [128, 129], F32) for d in range(2)]
            TMPr, NUMr, DSr, CSr = [cx.regs(2) for _ in range(4)]
            JK = [sb("jk%d" % j, [128, 128], BF16) for j in range(2)]
            HSt = sb("hst_stat", [128, NT, 2], F32)
            HR = sb("hr", [128, NT, 2], F32)
            SG = [sb("sg%d" % j, [128, 256], F32) for j in range(2)]
            HSB = [sb("hsb%d" % j, [128, 256], BF16) for j in range(2)]
            HST = [sb("hstt%d" % j, [128, 2, 128], BF16) for j in range(2)]
            JKr, SGr, HSBr, HSTr = cx.regs(2), cx.regs(2), cx.regs(2), cx.regs(2)
            hsr = cx.reg("hstat")

            import os as _os
            _stop = _os.environ.get("MLSTM_STOP", "all")
            for p in range(int(_os.environ.get("MLSTM_NP", "4")) if _stop != "gates" else 0):
                h0 = 2 * p
                hsl = slice(h0, h0 + 2)
                cnt = [0]

                def nxt():
                    j = cnt[0] % 4
                    cnt[0] += 1
                    return j
                def kv_tile(i):
                    j = nxt()
                    for k in range(KC):
                        cx.op(cx.pe, lambda k=k: nc.tensor.matmul(PSB[j][:, 256:384], self.UT[:, k, i * 128:(i + 1) * 128], WQK[:, k, 128:256],
                                                                  start=(k == 0), stop=(k == KC - 1)),
                              reads=[wr1, self.UTr[i]], writes=[PSr[j]], inc=False)
                    for k in range(KC):
                        cx.op(cx.pe, lambda k=k: nc.tensor.matmul(PSB[j][:, 0:256], self.UT[:, k, i * 128:(i + 1) * 128], WV[:, k, :],
                                                                  start=(k == 0), stop=(k == KC - 1)),
                              reads=[wr1, self.UTr[i]], writes=[PSr[j]], inc=(k == KC - 1))
                    cx.op(cx.act, lambda: nc.scalar.copy(out=KTOK[:, i, :], in_=PSB[j][:, 256:384]), writes=[PSr[j], KTOKr[i]])
                    cx.op(cx.dve, lambda: nc.vector.tensor_copy(out=VA[:, i, :, 0:128], in_=PSB[j][:, 0:256].rearrange("p (h v) -> p h v", h=2)),
                          writes=[PSr[j], Vr[i]])
                def kw_tile(i):
                    for d in range(2):
                        cx.op(cx.dve, lambda d=d: nc.vector.tensor_tensor(
                            out=KWall[:, i, d, :].rearrange("p (h k) -> p h k", h=2), in0=KTOK[:, i, :].rearrange("p (h k) -> p h k", h=2),
                            in1=WKK[:, i, d, hsl].unsqueeze(2).broadcast_to([128, 2, 64]), op=ALU.mult),
                            reads=[KTOKr[i], gr], writes=[KWr[d][i]])
                for hp in range(2):
                    cx.op(cx.dve, lambda hp=hp: nc.vector.tensor_copy(out=GXP[hp * 64:(hp + 1) * 64, :, :], in_=GX[hp * 64:(hp + 1) * 64, :, :, h0 + hp]),
                          reads=[gr], writes=[gxpr])
                def qk_group(which, tb):
                        j = nxt()
                        for k in range(KC):
                            cx.op(cx.pe, lambda k=k: nc.tensor.matmul(PSB[j][:], WQK[:, k, which * 128:(which + 1) * 128],
                                                                      self.UT[:, k, tb * 512:(tb + 1) * 512], start=(k == 0), stop=(k == KC - 1)),
                                  reads=[wr1] + self.UTr[tb * 4:tb * 4 + 4], writes=[PSr[j]], inc=(k == KC - 1))
                        if which == 0:
                            for hp in range(2):
                                cx.op(cx.act, lambda hp=hp: nc.scalar.mul(out=QTbd[hp * 64:(hp + 1) * 64, hp, tb * 512:(tb + 1) * 512],
                                                                          in_=PSB[j][hp * 64:(hp + 1) * 64, :], mul=DK ** -0.5),
                                      writes=[PSr[j], QTr[tb]])
                        else:
                            cx.op(cx.dve, lambda: nc.vector.tensor_copy(out=KT2[:, tb * 512:(tb + 1) * 512], in_=PSB[j][:]),
                                  writes=[PSr[j], KTr[tb]])
                DB = [6, 7]
                for d in range(2):
                    cx.op(cx.dve, lambda d=d: nc.vector.memset(CS[d][:], 0.0), writes=[CSr[d]])

                def emit_dc(stp, d):
                    c = stp if d == 0 else NT - 1 - stp
                    off = (stp % 2) * 256
                    for hp in range(2):
                        cx.op(cx.pe, lambda hp=hp: nc.tensor.matmul(PSB[DB[d]][hp * 64:(hp + 1) * 64, off:off + 129],
                                                                    KWall[:, c, d, hp * 64:(hp + 1) * 64], VA[:, c, hp, :], start=True, stop=True),
                              reads=[KWr[d][c], Vr[c], var1], writes=[PSr[DB[d]]], inc=(hp == 1))
                def s_step(stp):
                    for d in range(2):
                        c = stp if d == 0 else NT - 1 - stp
                        cn = c + 1 if d == 0 else c - 1
                        off = (stp % 2) * 256
                        if stp + 1 < NT - 1:
                            emit_dc(stp + 1, d)
                        cx.op(cx.dve, lambda: nc.vector.scalar_tensor_tensor(out=CS[d][:], in0=CS[d][:], scalar=GXP[:, c, d:d + 1],
                                                                             in1=PSB[DB[d]][:, off:off + 129], op0=ALU.mult, op1=ALU.add),
                              reads=[gxpr], writes=[PSr[DB[d]], CSr[d]])
                        cx.op(cx.act, lambda: nc.scalar.copy(out=CBall[d][:, cn, :], in_=CS[d][:]), reads=[CSr[d]], writes=[CBAr[d][cn]])
                for s8 in range(8):
                    for i in (s8, NT - 1 - s8):
                        kv_tile(i)
                        kw_tile(i)
                    if s8 == 0:
                        for d in range(2):
                            emit_dc(0, d)
                    if s8 >= 1:
                        s_step(s8 - 1)
                s_step(7)
                for g8 in range(8):
                    qk_group(g8 // 4, g8 % 4)
                    if 8 + g8 < NT - 1:
                        s_step(8 + g8)
                IB = [2, 4]
                NB_ = [3, 5]
                def emitA(c):
                    a = c % 2
                    tok = slice(c * 128, (c + 1) * 128)
                    tbk = c // 4
                    eb, sbk = a, 6 + a
                    cx.op(cx.pool, lambda: nc.gpsimd.tensor_tensor(out=LFM[a][:], in0=TRI2.unsqueeze(2).broadcast_to([128, 2, 2, 128]),
                                                                   in1=LF[:, c, :, hsl].unsqueeze(3).broadcast_to([128, 2, 2, 128]), op=ALU.mult),
                          reads=[gr, self.cr], writes=[LFMr[a]])
                    for d in range(2):
                        cx.op(cx.pe, lambda d=d: nc.tensor.matmul(PSB[eb][:, d * 256:(d + 1) * 256], MU[d], LFM[a][:, d], start=True, stop=False),
                              reads=[LFMr[a], self.cr], writes=[PSr[eb]], inc=False)
                        cx.op(cx.pe, lambda d=d: nc.tensor.matmul(PSB[eb][:, d * 256:(d + 1) * 256], self.ident, NEG4[:, d], start=False, stop=True),
                              reads=[negr, self.cr], writes=[PSr[eb]], inc=(d == 1))
                    cx.op(cx.pe, lambda: nc.tensor.matmul(PSB[sbk][:, 0:256], KT2[:, tok], QTbd[:, :, tok], start=True, stop=True),
                          reads=[KTr[tbk], QTr[tbk]], writes=[PSr[sbk]])
                    E4 = PSB[eb][:].rearrange("p (d h t) -> p d h t", d=2, h=2)
                    for d in range(2):
                        for hp in range(2):
                            cx.op(cx.act, lambda d=d, hp=hp: nc.scalar.activation(out=AT[a][:, d, hp, :], in_=E4[:, d, hp, :], func=AF.Exp,
                                                                                 bias=LI[:, c, d, h0 + hp:h0 + hp + 1]),
                                  reads=[gr], writes=[PSr[eb], ATr[a]])
                    cx.op(cx.dve, lambda: nc.vector.tensor_tensor(
                        out=PTt[a][:], in0=PSB[sbk][:, 0:256].rearrange("p (h t) -> p h t", h=2).unsqueeze(1).broadcast_to([128, 2, 2, 128]),
                        in1=AT[a][:], op=ALU.mult),
                        reads=[ATr[a]], writes=[PSr[sbk], PTr[a]])

                def emitB(c):
                    a = c % 2
                    tok = slice(c * 128, (c + 1) * 128)
                    tbk = c // 4
                    eb, sbk = a, 6 + a
                    for d in range(2):
                        for hp in range(2):
                            cx.op(cx.pe, lambda hp=hp: nc.tensor.matmul(PSB[IB[d]][:, hp * 256:hp * 256 + 129], PTt[a][:, d, hp, :], VA[:, c, hp, :],
                                                                        start=True, stop=True),
                                  reads=[PTr[a], Vr[c], var1], writes=[PSr[IB[d]]], inc=(hp == 1))
                        for hp in range(2):
                            cx.op(cx.pe, lambda hp=hp: nc.tensor.matmul(PSB[NB_[d]][:, hp * 256:hp * 256 + 129], QTbd[:, hp, tok],
                                                                        CBall[d][:, c, :], start=True, stop=True),
                                  reads=[QTr[tbk], CBAr[d][c]], writes=[PSr[NB_[d]]], inc=(hp == 1))
                    for d in range(2):
                        iv = PSB[NB_[d]][:].rearrange("p (h x) -> p h x", h=2)[:, :, 0:129]
                        nv = PSB[IB[d]][:].rearrange("p (h x) -> p h x", h=2)[:, :, 0:129]
                        for hp in range(2):
                            cx.op(cx.act, lambda hp=hp: nc.scalar.mul(out=TMP[d][:, hp, :], in_=iv[:, hp, :],
                                                                      mul=G[:, c, d, h0 + hp:h0 + hp + 1]),
                                  reads=[gr], writes=[PSr[NB_[d]], TMPr[d]])
                        cx.op(cx.dve, lambda: nc.vector.tensor_tensor(out=NUM[d], in0=nv, in1=TMP[d][:], op=ALU.add),
                              reads=[TMPr[d]], writes=[PSr[IB[d]], NUMr[d]])
                    den = NUMt[:, :, :, 128]
                    dv = lambda lo: DSt[:, lo:lo + 4].rearrange("p (d h) -> p d h", d=2)
                    cx.op(cx.dve, lambda: nc.vector.scalar_tensor_tensor(out=dv(0), in0=den, scalar=-1.0, in1=den, op0=ALU.mult, op1=ALU.max),
                          reads=[NUMr[0], NUMr[1]], writes=[DSr[0]])
                    cx.op(cx.dve, lambda: nc.vector.tensor_scalar(out=DSt[:, 0:4], in0=DSt[:, 0:4], scalar1=1.0, scalar2=None, op0=ALU.max),
                          writes=[DSr[0]])
                    cx.op(cx.dve, lambda: nc.vector.reciprocal(out=DSt[:, 4:8], in_=DSt[:, 0:4]), writes=[DSr[0]])
                    h3 = H32[:, c, :].rearrange("p (h v) -> p h v", h=2)
                    cx.op(cx.dve, lambda: nc.vector.tensor_tensor(out=h3, in0=NUMt[:, 0, :, 0:128],
                                                                  in1=DSt[:, 4:6].unsqueeze(2).broadcast_to([128, 2, 128]), op=ALU.mult),
                          reads=[NUMr[0], DSr[0]], writes=[Hr[c]])
                    for hp in range(2):
                        cx.op(cx.dve, lambda hp=hp: nc.vector.scalar_tensor_tensor(
                            out=h3[:, hp, :], in0=NUMt[:, 1, hp, 0:128], scalar=DSt[:, 6 + hp:7 + hp], in1=h3[:, hp, :],
                            op0=ALU.mult, op1=ALU.add),
                            reads=[NUMr[1], DSr[0]], writes=[Hr[c]])
                def stat_tile(i):
                    for hp in range(2):
                        cx.op(cx.act, lambda hp=hp: nc.scalar.activation(out=JK[hp][:], in_=H32[:, i, hp * 128:(hp + 1) * 128], func=AF.Square,
                                                                         accum_out=HSt[:, i, hp:hp + 1]),
                              reads=[Hr[i]], writes=[JKr[hp], hsr])
                emitA(0)
                for c in range(NT):
                    if c + 1 < NT:
                        emitA(c + 1)
                    if c == NT - 2 and p + 1 < 4:
                        load_w1(p + 1)
                    emitB(c)
                for i in range(NT):
                    stat_tile(i)
                cx.op(cx.dve, lambda: nc.vector.tensor_scalar(out=HR[:], in0=HSt[:], scalar1=1.0 / DV, scalar2=EPS, op0=ALU.mult, op1=ALU.add),
                      writes=[hsr])
                cx.op(cx.act, lambda: nc.scalar.activation(out=HR[:], in_=HR[:], func=AF.Sqrt), writes=[hsr])
                cx.op(cx.dve, lambda: nc.vector.reciprocal(out=HR[:], in_=HR[:]), writes=[hsr])
                for g4 in range(4):
                    sl = slice(g4 * 4, g4 * 4 + 4)
                    hv = H32[:, sl, :].rearrange("p c (h v) -> p c h v", h=2)
                    cx.op(cx.dve, lambda: nc.vector.tensor_tensor(out=hv, in0=hv, in1=HR[:, sl, :].unsqueeze(3).broadcast_to([128, 4, 2, 128]),
                                                                  op=ALU.mult), reads=[hsr], writes=Hr[sl])

                def st_o(i):
                    b = i % 2
                    j = i % 2
                    for k in range(KC):
                        cx.op(cx.pe, lambda k=k: nc.tensor.matmul(PSB[j][:, 0:256], self.UT[:, k, i * 128:(i + 1) * 128], WO[:, k, :],
                                                                  start=(k == 0), stop=(k == KC - 1)),
                              reads=[wr2, self.UTr[i]], writes=[PSr[j]], inc=(k == KC - 1))
                    cx.op(cx.act, lambda: nc.scalar.activation(out=SG[b][:], in_=PSB[j][:, 0:256], func=AF.Sigmoid), writes=[PSr[j], SGr[b]])
                    cx.op(cx.pool, lambda: nc.gpsimd.tensor_tensor(out=SG[b][:], in0=SG[b][:], in1=HGB[:], op=ALU.mult),
                          reads=[wr2], writes=[SGr[b]])
                    cx.op(cx.pool, lambda: nc.gpsimd.tensor_tensor(out=HSB[b][:], in0=H32[:, i, :], in1=SG[b][:], op=ALU.mult),
                          reads=[Hr[i], SGr[b]], writes=[HSBr[b]])

                def st_t(i):
                    b = i % 2
                    j = 2 + i % 2
                    for hp in range(2):
                        cx.op(cx.pe, lambda hp=hp: nc.tensor.matmul(PSB[j][:, hp * 128:(hp + 1) * 128], HSB[b][:, hp * 128:(hp + 1) * 128], self.ident,
                                                                    start=True, stop=True),
                              reads=[HSBr[b], self.cr], writes=[PSr[j]], inc=(hp == 1))
                    cx.op(cx.act, lambda: nc.scalar.copy(out=HST[b][:], in_=PSB[j][:, 0:256].rearrange("p (h t) -> p h t", h=2)),
                          writes=[PSr[j], HSTr[b]])

                def st_x(i):
                    b = i % 2
                    for nh in range(2):
                        j = 4 + (i * 2 + nh) % 4
                        for hp in range(2):
                            cx.op(cx.pe, lambda hp=hp: nc.tensor.matmul(PSB[j][:], HST[b][:, hp, :], WOUT[:, hp, nh * 512:(nh + 1) * 512],
                                                                        start=(hp == 0), stop=(hp == 1)),
                                  reads=[HSTr[b], wr2], writes=[PSr[j]], inc=(hp == 1))
                        cx.op(cx.dve, lambda: nc.vector.tensor_tensor(out=self.X[:, i, nh * 512:(nh + 1) * 512], in0=PSB[j][:],
                                                                      in1=self.X[:, i, nh * 512:(nh + 1) * 512], op=ALU.add),
                              writes=[PSr[j], self.Xr[i]])
                for n in range(NT + 2):
                    if n < NT:
                        st_o(n)
                    if 1 <= n < NT + 1:
                        st_t(n - 1)
                    if n >= 2:
                        st_x(n - 2)
                if p + 1 < 4:
                    load_w2(p + 1)

    def pool_mixer(self, g_row):
        nc, cx = self.nc, self.cx
        winv = self.p_win.rearrange("(k p) n -> p k n", p=128)
        woutv = self.p_wout.rearrange("(k p) n -> p k n", p=128)
        wgv = self.p_wg.rearrange("g (c p) n -> p (g c) n", p=128)
        with ExitStack() as es:
            WI = self.sb(es, "pm_wi", [128, KC, D], BF16)
            WG = self.sb(es, "pm_wg", [128, KC, 256], BF16)
            WO = WI
            SCB = self.sb(es, "pm_scb", [128, D], F32)
            PB16 = self.sb(es, "pm_pb16", [128, 20 * 128], BF16)
            pbr = cx.reg("pb16")
            WIr, WGr, SCr = cx.reg(), cx.reg(), cx.reg()
            WOr = WIr
            dsw = [cx.dsem("ds_pm%d" % j) for j in range(4)]
            cx.dma(cx.pool, cx.dsem("ds_pb16"), lambda: nc.gpsimd.dma_start(out=PB16[:], in_=self.c16_d[:, C16_POOL:C16_POOL + 20 * 128]), writes=[pbr])
            cx.dma(cx.pool, dsw[0], lambda: nc.gpsimd.dma_start(out=WI[:], in_=winv), writes=[WIr])
            cx.dma(cx.pool, dsw[1], lambda: nc.gpsimd.dma_start(out=WG[:], in_=wgv), writes=[WGr])
            A = self.sb(es, "pm_a", [128, NT, D], BF16)
            PT = self.sb(es, "pm_pt", [128, KC, S], BF16)
            TMPs = [self.sb(es, "pm_tmp%d" % j, [128, 512], F32) for j in range(2)]
            Ar = cx.regs(NT, "A")
            PTr = [[cx.reg() for _ in range(4)] for _ in range(KC)]
            TMr = cx.regs(2)
            PS = [self.ps(es, "pm_ps%d" % j, [128, 512], F32) for j in range(4)]
            PSr = cx.regs(4)
            self.norm_transpose(g_row, NB=2, gb=SCB)
            SCr = self.GBr
            cx.dma(cx.sp, dsw[3], lambda: nc.sync.dma_start(out=SCB[:], in_=self.p_scale[0, :].partition_broadcast(128)),
                   writes=[SCr])
            cnt = [0]

            def nxt():
                j = cnt[0] % 4
                cnt[0] += 1
                return j
            for i in range(NT):
                for nh in range(2):
                    j = nxt()
                    for k in range(KC):
                        cx.op(cx.pe, lambda k=k: nc.tensor.matmul(
                            PS[j][:], self.UT[:, k, i * 128:(i + 1) * 128], WI[:, k, nh * 512:(nh + 1) * 512],
                            start=(k == 0), stop=(k == KC - 1)),
                            reads=[WIr, self.UTr[i]], writes=[PSr[j]], inc=(k == KC - 1))
                    cx.op(cx.act, lambda: nc.scalar.copy(out=A[:, i, nh * 512:(nh + 1) * 512], in_=PS[j][:]),
                          reads=[PSr[j]], writes=[Ar[i]])
            cx.dma(cx.pool, dsw[2], lambda: nc.gpsimd.dma_start(out=WO[:], in_=woutv), writes=[WOr])
            for cc in range(KC):
                wi = cc // 2

                def blk(kind):
                    o = (wi * 5 + POOL_KINDS.index(kind)) * 128
                    return PB16[:, o:o + 128]
                for tb in range(4):
                    j = nxt()
                    for ii in range(4):
                        i = tb * 4 + ii
                        terms = []
                        if i > 0:
                            terms.append((i - 1, "sub"))
                        terms.append((i, "first" if i == 0 else ("last" if i == NT - 1 else "diag")))
                        if i < NT - 1:
                            terms.append((i + 1, "sup"))
                        for n, (si, kind) in enumerate(terms):
                            cx.op(cx.pe, lambda si=si, kind=kind, n=n: nc.tensor.matmul(
                                PS[j][:, ii * 128:(ii + 1) * 128], A[:, si, cc * 128:(cc + 1) * 128], blk(kind),
                                start=(n == 0), stop=(n == len(terms) - 1)),
                                reads=[Ar[si], pbr], writes=[PSr[j]], inc=(ii == 3 and n == len(terms) - 1))
                    cx.op(cx.act, lambda: nc.scalar.copy(out=PT[:, cc, tb * 512:(tb + 1) * 512], in_=PS[j][:]),
                          reads=[PSr[j]], writes=[PTr[cc][tb]])
            for g in range(4):
                for dc in range(2):
                    for tb in range(4):
                        j = nxt()
                        for ci in range(2):
                            cx.op(cx.pe, lambda ci=ci: nc.tensor.matmul(
                                PS[j][:], WG[:, g * 2 + ci, dc * 128:(dc + 1) * 128], PT[:, g * 2 + ci, tb * 512:(tb + 1) * 512],
                                start=(ci == 0), stop=(ci == 1)),
                                reads=[WGr, PTr[g * 2 + ci][tb]], writes=[PSr[j]], inc=(ci == 1))
                        cx.op(cx.act, lambda: nc.scalar.copy(out=self.UT[:, g * 2 + dc, tb * 512:(tb + 1) * 512], in_=PS[j][:]),
                              reads=[PSr[j]], writes=self.UTr[tb * 4:tb * 4 + 4])
            for i in range(NT):
                for nh in range(2):
                    j = nxt()
                    b = (i * 2 + nh) % 2
                    for k in range(KC):
                        cx.op(cx.pe, lambda k=k: nc.tensor.matmul(
                            PS[j][:], self.UT[:, k, i * 128:(i + 1) * 128], WO[:, k, nh * 512:(nh + 1) * 512],
                            start=(k == 0), stop=(k == KC - 1)),
                            reads=[WOr, self.UTr[i]], writes=[PSr[j]], inc=(k == KC - 1))
                    cx.op(cx.dve, lambda: nc.vector.tensor_tensor(out=TMPs[b][:], in0=PS[j][:], in1=SCB[:, nh * 512:(nh + 1) * 512],
                                                                  op=ALU.mult),
                          reads=[PSr[j], SCr], writes=[TMr[b]])
                    cx.op(cx.pool, lambda: nc.gpsimd.tensor_tensor(
                        out=self.X[:, i, nh * 512:(nh + 1) * 512], in0=self.X[:, i, nh * 512:(nh + 1) * 512], in1=TMPs[b][:],
                        op=ALU.add),
                        reads=[TMr[b], self.Xr[i]], writes=[self.Xr[i]])


_W_NAMES = ["mix_norm_g", "mlp_norm_g", "mlstm_w_in", "mlstm_gate_b", "mlstm_head_g", "mlstm_w_out", "pool_w_in",
            "pool_w_group", "pool_scale", "pool_w_out", "mlp_w1", "mlp_w2", "final_norm_g"]


def _in_maps(inputs, xs):
    c16, c32 = make_consts()
    base = {
        "mix_norm_g": inputs["mix_norm_g"], "mlp_norm_g": inputs["mlp_norm_g"],
        "mlstm_w_in": inputs["mlstm_w_in"][0], "mlstm_gate_b": inputs["mlstm_gate_b"],
        "mlstm_head_g": inputs["mlstm_head_g"], "mlstm_w_out": inputs["mlstm_w_out"][0],
        "pool_w_in": inputs["pool_w_in"][0], "pool_w_group": inputs["pool_w_group"][0],
        "pool_w_out": inputs["pool_w_out"][0], "pool_scale": inputs["pool_scale"],
        "mlp_w1": inputs["mlp_w1"], "mlp_w2": inputs["mlp_w2"],
        "final_norm_g": inputs["final_norm_g"].reshape(1, D),
        "c16": c16, "c32": c32,
    }
    base = {k: np.ascontiguousarray(v, dtype=np.float32) for k, v in base.items()}
    return [dict(base, x=np.ascontiguousarray(x, dtype=np.float32)) for x in xs]


def run(inputs, xs, phases=None, final=True, trace=False):
    nc = Builder(phases, final).build()
    res = run_bass_kernel_spmd(nc, _in_maps(inputs, xs), core_ids=list(range(len(xs))), trace=trace)
    return [r["y"] for r in res.results], res


def kernel(**inputs):
    x = np.asarray(inputs["x"], dtype=np.float32)
    ys, _ = run(inputs, [x[b] for b in range(x.shape[0])])
    return np.stack(ys, axis=0).astype(np.float32)
```
